# Optimizing a Trainium2 kernel written in Bass

```python
import math
import jax, jax.numpy as jnp
from jax import lax
import numpy as np

D_MODEL = 1024
BATCH = 8
SEQ = 2048
DEPTH = 2
DEC_BATCH = 128
DEC_SEQ = 4
PAST_LEN = 8192
PAGE_SIZE = 128

N_HEADS = 8
QK_NOPE_DIM = 128
QK_ROPE_DIM = 64
V_HEAD_DIM = 128
Q_LORA_RANK = 384
KV_LORA_RANK = 256
ROPE_THETA = 10000.0
Q_BLOCK = 128
D_SSM = D_MODEL
SSM_GROUP = 16
N_GROUPS = D_SSM // SSM_GROUP
SSM_STATE = 64
DT_MIN = 1e-3
DT_MAX = 1e-1
D_FF = 2816
N_MOD = 9
EPS = 1e-6
IN_SPLITS = [Q_LORA_RANK,
             Q_LORA_RANK + KV_LORA_RANK,
             Q_LORA_RANK + KV_LORA_RANK + QK_ROPE_DIM,
             Q_LORA_RANK + KV_LORA_RANK + QK_ROPE_DIM + D_SSM,
             Q_LORA_RANK + KV_LORA_RANK + QK_ROPE_DIM + D_SSM + D_MODEL]
IN_WIDTH = Q_LORA_RANK + KV_LORA_RANK + QK_ROPE_DIM + D_SSM + 2 * D_MODEL

kernel_name = "mla_s5_gated_macaron_decode_step"


def rms_norm(x, g):
    xf = x.astype(jnp.float32)
    y = xf * lax.rsqrt(jnp.mean(xf * xf, axis=-1, keepdims=True) + EPS)
    return (y * g.astype(jnp.float32)).astype(x.dtype)


def rope(x, pos):
    half = x.shape[-1] // 2
    inv = ROPE_THETA ** (-jnp.arange(half, dtype=jnp.float32) / half)
    ang = pos.astype(jnp.float32)[:, None] * inv[None, :]
    shp = (ang.shape[0],) + (1,) * (x.ndim - 3) + (half,)
    cos = jnp.cos(ang).reshape(shp)
    sin = jnp.sin(ang).reshape(shp)
    xf = x.astype(jnp.float32)
    x1, x2 = xf[..., :half], xf[..., half:]
    return jnp.concatenate([x1 * cos - x2 * sin, x1 * sin + x2 * cos], axis=-1).astype(x.dtype)


def swiglu(h, w_up, w_down):
    a, b = jnp.split(h @ w_up, 2, axis=-1)
    return (jax.nn.silu(a) * b) @ w_down


def attend_prompt(q_lat, q_rope, ckv, kr):
    bsz, t, h, lat = q_lat.shape
    nb = t // Q_BLOCK
    ql = q_lat.reshape(bsz, nb, Q_BLOCK, h, lat).transpose(1, 0, 2, 3, 4)
    qr = q_rope.reshape(bsz, nb, Q_BLOCK, h, QK_ROPE_DIM).transpose(1, 0, 2, 3, 4)
    k_pos = jnp.arange(t)

    def one_block(args):
        ql_b, qr_b, i = args
        s = (jnp.einsum('bqhl,bsl->bhqs', ql_b, ckv)
             + jnp.einsum('bqhr,bsr->bhqs', qr_b, kr)).astype(jnp.float32)
        q_pos = i * Q_BLOCK + jnp.arange(Q_BLOCK)
        mask = k_pos[None, :] <= q_pos[:, None]
        p = jax.nn.softmax(jnp.where(mask, s, -jnp.inf), axis=-1).astype(ckv.dtype)
        return jnp.einsum('bhqs,bsl->bqhl', p, ckv)

    o = lax.map(one_block, (ql, qr, jnp.arange(nb)))
    return o.transpose(1, 0, 2, 3, 4).reshape(bsz, t, h, lat)


def attend_sample(q_lat, q_rope, ckv, kr, ckv_past, kr_past):
    t = q_lat.shape[1]
    past = ckv_past.shape[1]
    s_past = (jnp.einsum('bthl,bsl->bhts', q_lat, ckv_past)
              + jnp.einsum('bthr,bsr->bhts', q_rope, kr_past)).astype(jnp.float32)
    s_new = (jnp.einsum('bthl,bsl->bhts', q_lat, ckv)
             + jnp.einsum('bthr,bsr->bhts', q_rope, kr)).astype(jnp.float32)
    causal = jnp.tril(jnp.ones((t, t), dtype=bool))
    s_new = jnp.where(causal, s_new, -jnp.inf)
    p = jax.nn.softmax(jnp.concatenate([s_past, s_new], axis=-1), axis=-1).astype(ckv.dtype)
    return (jnp.einsum('bhts,bsl->bthl', p[..., :past], ckv_past)
            + jnp.einsum('bhts,bsl->bthl', p[..., past:], ckv))


def _diag_combine(e1, e2):
    a1, b1 = e1
    a2, b2 = e2
    return a1 * a2, a2 * b1 + b2


def s5_branch(u, a_re, a_im, log_dt, b_re, b_im, c_re, c_im, d_skip, w_glu, h0_re, h0_im):
    bsz, t, _ = u.shape
    f32 = jnp.float32
    uf = u.astype(f32).reshape(bsz, t, N_GROUPS, SSM_GROUP)
    lam = lax.complex(a_re.astype(f32), a_im.astype(f32))
    lam_dt = lam * jnp.exp(log_dt.astype(f32))[:, None]
    lam_bar = jnp.exp(lam_dt)
    b_bar = ((lam_bar - 1.0) / lam)[..., None] * lax.complex(b_re.astype(f32), b_im.astype(f32))
    bu = lax.complex(jnp.einsum('gpc,btgc->tbgp', b_bar.real, uf),
                     jnp.einsum('gpc,btgc->tbgp', b_bar.imag, uf))
    a = jnp.broadcast_to(lam_bar, (t, 1) + lam_bar.shape)
    _, hs = lax.associative_scan(_diag_combine, (a, bu), axis=0)
    if h0_re is not None:
        steps = jnp.arange(1, t + 1, dtype=f32)[:, None, None, None]
        h0 = lax.complex(h0_re.astype(f32), h0_im.astype(f32))
        hs = hs + jnp.exp(lam_dt[None, None] * steps) * h0[None]
    y = (jnp.einsum('gcp,tbgp->btgc', c_re.astype(f32), hs.real)
         - jnp.einsum('gcp,tbgp->btgc', c_im.astype(f32), hs.imag))
    y = (y + d_skip.astype(f32).reshape(N_GROUPS, SSM_GROUP) * uf).reshape(bsz, t, D_SSM)
    z = jax.nn.gelu(y).astype(u.dtype)
    ga, gb = jnp.split(z @ w_glu, 2, axis=-1)
    h_last = hs[-1]
    return ga * jax.nn.sigmoid(gb), h_last.real, h_last.imag


def layer(x, c, pos, past_ckv, past_kr, h0_re, h0_im,
          ada_w, ada_b, norm_ffn1, ffn1_up, ffn1_down, norm_mix, w_in,
          q_norm, w_uq, kv_norm, w_uk, w_uv,
          ssm_a_re, ssm_a_im, ssm_log_dt, ssm_b_re, ssm_b_im, ssm_c_re, ssm_c_im, ssm_d, w_glu,
          w_out, norm_ffn2, ffn2_up, ffn2_down):
    bsz, t, _ = x.shape
    mod = (jax.nn.silu(c) @ ada_w + ada_b)[:, None, :]
    sh1, sc1, g1, shm, scm, gm, sh2, sc2, g2 = jnp.split(mod, N_MOD, axis=-1)
    h = rms_norm(x, norm_ffn1) * (1.0 + sc1) + sh1
    x = x + 0.5 * g1 * swiglu(h, ffn1_up, ffn1_down)
    h = rms_norm(x, norm_mix) * (1.0 + scm) + shm
    cq, ckv, kr, u, gate_a, gate_b = jnp.split(h @ w_in, IN_SPLITS, axis=-1)
    q = (rms_norm(cq, q_norm) @ w_uq).reshape(bsz, t, N_HEADS, QK_NOPE_DIM + QK_ROPE_DIM)
    scale = (QK_NOPE_DIM + QK_ROPE_DIM) ** -0.5
    q_lat = jnp.einsum('bthn,lhn->bthl', q[..., :QK_NOPE_DIM], w_uk) * scale
    q_rope = rope(q[..., QK_NOPE_DIM:], pos) * scale
    ckv = rms_norm(ckv, kv_norm)
    kr = rope(kr, pos)
    if past_ckv is None:
        o_lat = attend_prompt(q_lat, q_rope, ckv, kr)
    else:
        o_lat = attend_sample(q_lat, q_rope, ckv, kr, past_ckv, past_kr)
    y_attn = jnp.einsum('bthl,lhv->bthv', o_lat, w_uv).reshape(bsz, t, N_HEADS * V_HEAD_DIM)
    y_ssm, h_re, h_im = s5_branch(u, ssm_a_re, ssm_a_im, ssm_log_dt, ssm_b_re, ssm_b_im,
                                  ssm_c_re, ssm_c_im, ssm_d, w_glu, h0_re, h0_im)
    mixed = jax.nn.sigmoid(gate_a) * y_attn + jax.nn.sigmoid(gate_b) * y_ssm
    x = x + gm * (mixed @ w_out)
    h = rms_norm(x, norm_ffn2) * (1.0 + sc2) + sh2
    x = x + 0.5 * g2 * swiglu(h, ffn2_up, ffn2_down)
    return x, ckv, kr, h_re, h_im


def setup_inputs(seed: int = 0) -> dict:
    key = jax.random.key(seed)
    ks = iter(jax.random.split(key, 48))
    f32 = jnp.float32

    def nrm(shape, scale):
        return jax.random.normal(next(ks), shape, f32) * scale

    n_pages = PAST_LEN // PAGE_SIZE
    n_used = DEC_BATCH * n_pages
    n_phys = (5 * n_used + 3) // 4
    page_table = jax.random.permutation(next(ks), n_phys)[:n_used].reshape(DEC_BATCH, n_pages).astype(jnp.int32)
    a_im0 = jnp.pi * jnp.arange(SSM_STATE, dtype=f32)
    return {
        "x_prompt": nrm((BATCH, SEQ, D_MODEL), 1.0),
        "x_sample": nrm((DEC_BATCH, DEC_SEQ, D_MODEL), 1.0),
        "c_prompt": nrm((BATCH, D_MODEL), 1.0),
        "c_sample": nrm((DEC_BATCH, D_MODEL), 1.0),
        "cache_ckv": nrm((DEPTH, n_phys, PAGE_SIZE, KV_LORA_RANK), 1.0),
        "cache_kr": nrm((DEPTH, n_phys, PAGE_SIZE, QK_ROPE_DIM), 1.0),
        "state_ssm_re": nrm((DEPTH, DEC_BATCH, N_GROUPS, SSM_STATE), 0.3),
        "state_ssm_im": nrm((DEPTH, DEC_BATCH, N_GROUPS, SSM_STATE), 0.3),
        "page_table": page_table,
        "ada_w": nrm((DEPTH, D_MODEL, N_MOD * D_MODEL), 0.5 * D_MODEL ** -0.5),
        "ada_b": nrm((DEPTH, N_MOD * D_MODEL), 0.01),
        "norm_ffn1": 1.0 + nrm((DEPTH, D_MODEL), 0.02),
        "ffn1_up": nrm((DEPTH, D_MODEL, 2 * D_FF), D_MODEL ** -0.5),
        "ffn1_down": nrm((DEPTH, D_FF, D_MODEL), D_FF ** -0.5),
        "norm_mix": 1.0 + nrm((DEPTH, D_MODEL), 0.02),
        "w_in": nrm((DEPTH, D_MODEL, IN_WIDTH), D_MODEL ** -0.5),
        "q_norm": 1.0 + nrm((DEPTH, Q_LORA_RANK), 0.02),
        "w_uq": nrm((DEPTH, Q_LORA_RANK, N_HEADS * (QK_NOPE_DIM + QK_ROPE_DIM)), Q_LORA_RANK ** -0.5),
        "kv_norm": 1.0 + nrm((DEPTH, KV_LORA_RANK), 0.02),
        "w_uk": nrm((DEPTH, KV_LORA_RANK, N_HEADS, QK_NOPE_DIM), KV_LORA_RANK ** -0.5),
        "w_uv": nrm((DEPTH, KV_LORA_RANK, N_HEADS, V_HEAD_DIM), KV_LORA_RANK ** -0.5),
        "ssm_a_re": -0.5 + nrm((DEPTH, N_GROUPS, SSM_STATE), 0.01),
        "ssm_a_im": a_im0 + nrm((DEPTH, N_GROUPS, SSM_STATE), 0.01),
        "ssm_log_dt": jax.random.uniform(next(ks), (DEPTH, N_GROUPS), f32, math.log(DT_MIN), math.log(DT_MAX)),
        "ssm_b_re": nrm((DEPTH, N_GROUPS, SSM_STATE, SSM_GROUP), (2.0 * SSM_GROUP) ** -0.5),
        "ssm_b_im": nrm((DEPTH, N_GROUPS, SSM_STATE, SSM_GROUP), (2.0 * SSM_GROUP) ** -0.5),
        "ssm_c_re": nrm((DEPTH, N_GROUPS, SSM_GROUP, SSM_STATE), (2.0 * SSM_STATE) ** -0.5),
        "ssm_c_im": nrm((DEPTH, N_GROUPS, SSM_GROUP, SSM_STATE), (2.0 * SSM_STATE) ** -0.5),
        "ssm_d": nrm((DEPTH, D_SSM), 1.0),
        "w_glu": nrm((DEPTH, D_SSM, 2 * D_MODEL), D_SSM ** -0.5),
        "w_out": nrm((DEPTH, D_MODEL, D_MODEL), D_MODEL ** -0.5),
        "norm_ffn2": 1.0 + nrm((DEPTH, D_MODEL), 0.02),
        "ffn2_up": nrm((DEPTH, D_MODEL, 2 * D_FF), D_MODEL ** -0.5),
        "ffn2_down": nrm((DEPTH, D_FF, D_MODEL), D_FF ** -0.5),
        "final_norm": 1.0 + nrm((D_MODEL,), 0.02),
    }


def reference(x_prompt, x_sample, c_prompt, c_sample, cache_ckv, cache_kr, state_ssm_re, state_ssm_im,
              page_table, ada_w, ada_b, norm_ffn1, ffn1_up, ffn1_down, norm_mix, w_in, q_norm, w_uq,
              kv_norm, w_uk, w_uv, ssm_a_re, ssm_a_im, ssm_log_dt, ssm_b_re, ssm_b_im, ssm_c_re, ssm_c_im,
              ssm_d, w_glu, w_out, norm_ffn2, ffn2_up, ffn2_down, final_norm):
    dec_batch, n_pages = page_table.shape
    past_len = n_pages * cache_ckv.shape[2]
    pos_p = jnp.arange(x_prompt.shape[1])
    pos_s = past_len + jnp.arange(x_sample.shape[1])
    xp, xs = x_prompt, x_sample
    ckv_p, kr_p, hre_p, him_p = [], [], [], []
    ckv_s, kr_s, hre_s, him_s = [], [], [], []
    for l in range(DEPTH):
        lp = (ada_w[l], ada_b[l], norm_ffn1[l], ffn1_up[l], ffn1_down[l], norm_mix[l], w_in[l],
              q_norm[l], w_uq[l], kv_norm[l], w_uk[l], w_uv[l],
              ssm_a_re[l], ssm_a_im[l], ssm_log_dt[l], ssm_b_re[l], ssm_b_im[l], ssm_c_re[l], ssm_c_im[l],
              ssm_d[l], w_glu[l], w_out[l], norm_ffn2[l], ffn2_up[l], ffn2_down[l])
        xp, a1, a2, a3, a4 = layer(xp, c_prompt, pos_p, None, None, None, None, *lp)
        ckv_p.append(a1); kr_p.append(a2); hre_p.append(a3); him_p.append(a4)
        past_ckv = cache_ckv[l, page_table].reshape(dec_batch, past_len, KV_LORA_RANK)
        past_kr = cache_kr[l, page_table].reshape(dec_batch, past_len, QK_ROPE_DIM)
        xs, b1, b2, b3, b4 = layer(xs, c_sample, pos_s, past_ckv, past_kr,
                                   state_ssm_re[l], state_ssm_im[l], *lp)
        ckv_s.append(b1); kr_s.append(b2); hre_s.append(b3); him_s.append(b4)
    y_prompt = rms_norm(xp, final_norm)
    y_sample = rms_norm(xs, final_norm)
    return (y_prompt, y_sample,
            jnp.stack(ckv_p), jnp.stack(kr_p), jnp.stack(hre_p), jnp.stack(him_p),
            jnp.stack(ckv_s), jnp.stack(kr_s), jnp.stack(hre_s), jnp.stack(him_s))
```

```python
import contextlib
import math
import numpy as np
import concourse.bass as bass
import concourse.mybir as mybir
from concourse.bass_utils import run_bass_kernel_spmd

F32, BF16, I32 = mybir.dt.float32, mybir.dt.bfloat16, mybir.dt.int32
AF = mybir.ActivationFunctionType
ALU = mybir.AluOpType

D = 1024; KT = 8; DFF = 2816; FT = 22
TP = 2048; NSQ = 16; TS = 4; NSC = NSQ * TS
NH = 8; LAT = 256; ROPE = 64; QL = 384
G = 64; PST = 64
PAST = 8192; PAGE = 128; NPG = 64
NPHYS = 10240
INW = 3776
EPS = 1e-6
SCALE = (128 + 64) ** -0.5
OFF_CQ, OFF_CKV, OFF_KR, OFF_U, OFF_GA, OFF_GB = 0, 384, 640, 704, 1728, 2752


class Buf:
    __slots__ = ("w", "r", "dsem", "name")

    def __init__(self, name=""):
        self.w = None
        self.r = {}
        self.dsem = None
        self.name = name


class Tile:
    def __init__(self, t, name):
        self.t = t
        self.b = Buf(name)

    def __getitem__(self, k):
        return self.t[k]


class Fw:
    def __init__(self, nc, es):
        self.nc, self.es = nc, es
        self.eng = {"pe": nc.tensor, "act": nc.scalar, "dve": nc.vector, "pool": nc.gpsimd, "sp": nc.sync}
        self.sem = {k: es.enter_context(nc.semaphore("s_" + k)) for k in self.eng}
        self.cnt = {k: 0 for k in self.eng}
        self.waited = {k: {} for k in self.eng}
        self.dtot = {}
        self.dsems = []
        self.n = 0
        self.nins = 0

    def sb(self, shape, dt, name=None):
        self.n += 1
        name = name or ("t%d" % self.n)
        return Tile(self.es.enter_context(self.nc.sbuf_tensor(name, list(shape), dt)), name)

    def ps(self, shape, dt, name=None):
        self.n += 1
        name = name or ("p%d" % self.n)
        return Tile(self.es.enter_context(self.nc.psum_tensor(name, list(shape), dt)), name)

    def _wait(self, e, r, w):
        need = {}

        def add(ev):
            if ev is None:
                return
            s, v = ev
            k = id(s)
            if k in self.dtot:
                v = max(v, self.dtot[k][1])
            if k not in need or need[k][1] < v:
                need[k] = (s, v)

        for b in r:
            add(b.w)
        for b in w:
            add(b.w)
            for ev in b.r.values():
                add(ev)
        wd = self.waited[e]
        for k, (s, v) in need.items():
            if e == "pe" and s is self.sem["pe"]:
                continue
            if wd.get(k, 0) >= v:
                continue
            self.eng[e].wait_ge(s, v)
            wd[k] = v

    def _rec(self, ev, r, w):
        k = id(ev[0])
        for b in r:
            b.r[k] = ev
        for b in w:
            b.w = ev
            b.r = {}

    def op(self, e, fn, r=(), w=(), sig=True):
        r = [x.b if isinstance(x, Tile) else x for x in r]
        w = [x.b if isinstance(x, Tile) else x for x in w]
        self._wait(e, r, w)
        ins = fn(self.eng[e])
        if sig:
            self.cnt[e] += 1
            ins.then_inc(self.sem[e], 1)
            self._rec((self.sem[e], self.cnt[e]), r, w)
        else:
            self._rec((self.sem[e], self.cnt[e] + 1), r, w)
        self.nins += 1

    def dma(self, e, out, in_, owner, r=(), w=(), **kw):
        r = [x.b if isinstance(x, Tile) else x for x in r]
        w = [x.b if isinstance(x, Tile) else x for x in w]
        owner = owner.b if isinstance(owner, Tile) else owner
        self._wait(e, r, w)
        if owner.dsem is None:
            owner.dsem = self.es.enter_context(self.nc.semaphore("d%d" % len(self.dsems)))
            self.dsems.append(owner.dsem)
            self.dtot[id(owner.dsem)] = (owner.dsem, 0)
        s = owner.dsem
        tot = self.dtot[id(s)][1] + 16
        self.dtot[id(s)] = (s, tot)
        self.eng[e].dma_start(out=out, in_=in_, **kw).then_inc(s, 16)
        self._rec((s, tot), r, w)
        self.nins += 1

    def finish(self):
        for k, (s, v) in self.dtot.items():
            if v > 0:
                self.eng["sp"].wait_ge(s, v)
        for e in ("pe", "act", "dve", "pool"):
            if self.cnt[e] > 0:
                self.eng["sp"].wait_ge(self.sem[e], self.cnt[e])


class WStream:
    def __init__(self, fw, shape, nbuf, name):
        self.fw = fw
        self.bufs = [fw.sb(shape, BF16, "%s%d" % (name, i)) for i in range(nbuf)]
        self.nbuf = nbuf
        self.chunks = []
        self.issued = 0

    def add(self, fn):
        self.chunks.append(fn)
        return len(self.chunks) - 1

    def _issue(self, i):
        t = self.bufs[i % self.nbuf]
        for (o, a) in self.chunks[i](t):
            self.fw.dma("pool", o, a, owner=t, w=[t])

    def get(self, i, ahead=None):
        ahead = self.nbuf - 1 if ahead is None else ahead
        hi = min(len(self.chunks), i + 1 + ahead)
        while self.issued < hi:
            self._issue(self.issued)
            self.issued += 1
        return self.bufs[i % self.nbuf]


def build_program(cfg):
    nc = bass.Bass("TRN2", target_bir_lowering=False)
    es = contextlib.ExitStack()
    fw = Fw(nc, es)
    L_RANGE = cfg.get("layers", [0, 1])
    BLOCKS = cfg.get("blocks", [0, 1, 2])

    def din(name, shape, dt=F32):
        return nc.dram_tensor(name, list(shape), dt, kind="ExternalInput").ap()

    def dout(name, shape, dt=F32):
        return nc.dram_tensor(name, list(shape), dt, kind="ExternalOutput").ap()

    xp = din("xp", [TP, D]); xs = din("xs", [NSC, D]); cvec = din("cvec", [1 + NSQ, D])
    nphys = cfg.get("nphys", NPHYS)
    cache_ckv_l = [din("cache_ckv%d" % i, [nphys, PAGE, LAT]) for i in range(2)]
    cache_kr_l = [din("cache_kr%d" % i, [nphys, PAGE, ROPE]) for i in range(2)]
    st_re = din("st_re", [2, NSQ, G, PST]); st_im = din("st_im", [2, NSQ, G, PST])
    ptab = din("ptab", [NSQ, NPG], I32)
    ada_w = din("ada_w", [2, D, 9 * D]); ada_b = din("ada_b", [2, 9 * D])
    norm_ffn1 = din("norm_ffn1", [2, D]); ffn1_up = din("ffn1_up", [2, D, 2 * DFF]); ffn1_down = din("ffn1_down", [2, DFF, D])
    norm_mix = din("norm_mix", [2, D]); w_in = din("w_in", [2, D, INW])
    q_norm = din("q_norm", [2, QL]); w_uq = din("w_uq", [2, QL, NH * 192])
    kv_norm = din("kv_norm", [2, LAT]); w_uk = din("w_uk", [2, LAT, NH, 128]); w_uv = din("w_uv", [2, LAT, NH, 128])
    a_re = din("ssm_a_re", [2, G, PST]); a_im = din("ssm_a_im", [2, G, PST]); log_dt = din("ssm_log_dt", [2, G])
    b_re = din("ssm_b_re", [2, G, PST, 16]); b_im = din("ssm_b_im", [2, G, PST, 16])
    c_re = din("ssm_c_re", [2, G, 16, PST]); c_im = din("ssm_c_im", [2, G, 16, PST])
    ssm_d = din("ssm_d", [2, D]); w_glu = din("w_glu", [2, D, 2 * D]); w_out = din("w_out", [2, D, D])
    norm_ffn2 = din("norm_ffn2", [2, D]); ffn2_up = din("ffn2_up", [2, D, 2 * DFF]); ffn2_down = din("ffn2_down", [2, DFF, D])
    final_norm = din("final_norm", [D])
    ident_d = din("ident", [128, 128]); tri_d = din("tri", [128, 128])
    rope_d = din("rope", [2, 64, TP + NSC])
    mnew_d = din("mnew", [64, NSQ, 32]); ecol_d = din("ecol", [128, 1])
    s5c_d = din("s5c", [128, 870])

    y_p = dout("y_p", [TP, D]); y_s = dout("y_s", [NSC, D])
    ckv_p = dout("ckv_p", [2, TP, LAT]); kr_p = dout("kr_p", [2, TP, ROPE])
    hre_p = dout("hre_p", [2, G, PST]); him_p = dout("him_p", [2, G, PST])
    ckv_s = dout("ckv_s", [2, NSC, LAT]); kr_s = dout("kr_s", [2, NSC, ROPE])
    hre_s = dout("hre_s", [2, NSQ, G, PST]); him_s = dout("him_s", [2, NSQ, G, PST])

    NCB = 768
    BSTART = [0, 640, 1408]; BNPR = [640, 768, 640]; NCA = 1408
    ident = fw.sb([128, 128], F32, "ident_s"); identb = fw.sb([128, 128], BF16, "identb")
    trib = fw.sb([128, 128], BF16, "trib"); onesb = fw.sb([128, 128], BF16, "onesb")
    fw.dma("sp", ident[:], ident_d, owner=ident, w=[ident])
    fw.dma("pool", identb[:], ident_d, owner=identb, w=[identb])
    fw.dma("pool", trib[:], tri_d, owner=trib, w=[trib])
    fw.op("dve", lambda e: e.memset(onesb[:], 1.0), w=[onesb])
    onesf = fw.sb([1, 128], F32, "onesf"); npi = fw.sb([128, 1], F32, "npi")

    PS = [fw.ps([128, 512], F32, "ps%d" % i) for i in range(8)]
    psrr = [0]

    def psget(lo=0, hi=4):
        i = lo + psrr[0] % (hi - lo)
        psrr[0] += 1
        return PS[i]

    xT = [fw.sb([128, NCB], F32, "xT%d" % k) for k in range(KT)]
    NSLOT = 30
    slots = [fw.sb([128, NCB], BF16, "sl%d" % i) for i in range(NSLOT)]
    modT1 = fw.sb([128, 72, 17], F32, "modT"); modT = [modT1, modT1]
    vecT = [fw.sb([128, 128], F32, "vecT%d" % l) for l in range(2)]
    AB = [fw.sb([128, 6, KT, 17], F32, "AB%d" % l) for l in range(2)]
    GT = [fw.sb([128, 3, KT, 17], F32, "GT%d" % l) for l in range(2)]
    rstd = fw.sb([128, NCB], F32, "rstd")
    tmpf = [fw.sb([128, 512], F32, "tmpf%d" % i) for i in range(3)]
    tmpi = [0]

    def tmp():
        tmpi[0] += 1
        return tmpf[tmpi[0] % 3]

    sqb1 = fw.sb([128, 512], BF16, "sqb0"); sqb = [sqb1, sqb1]
    rope_cs = fw.sb([64, 2, NCB], F32, "rope_cs")
    kvT_A = [fw.sb([128, 2, NCA], BF16, "kvTA%d" % l) for l in range(2)]
    krT_A = [fw.sb([64, NCA], BF16, "krTA%d" % l) for l in range(2)]
    vtok_A = [fw.sb([128, 11, LAT], BF16, "vtokA%d" % l) for l in range(2)]
    kvT_L = fw.sb([128, 2, NCB], BF16, "kvTL"); krT_L = fw.sb([64, NCB], BF16, "krTL")
    vtok_L = fw.sb([128, 6, LAT], BF16, "vtokL")
    stage = [fw.sb([128, 1024], F32, "stage%d" % i) for i in range(2)]

    ws = WStream(fw, [128, 4096], 2, "ring")
    ws_up = ws_dn = ws_g = ws

    def vup(t):
        return t[:, 0:2048].rearrange("p (k c) -> p k c", c=256)

    def vdn(t):
        return t[:, 0:2816].rearrange("p (j c) -> p j c", c=128)

    def vg(t):
        return t[:, 0:4096].rearrange("p (k c) -> p k c", c=512)

    def vq(t):
        return t[:, 0:768].rearrange("p (k c) -> p k c", c=256)

    def vuk(t):
        return t[:, 0:2048].rearrange("p (c h n) -> p c h n", c=2, h=NH)
    wukT = fw.sb([128, NH, LAT], BF16, "wukT")
    wuv = fw.sb([128, 2, NH, 128], BF16, "wuv")

    def mm(out, lhsT, rhs, start, stop, r, w, lazy=False):
        fw.op("pe", lambda e: e.matmul(out, lhsT, rhs, start=start, stop=stop), r=r, w=w, sig=(bool(stop) or not lazy))

    def transpose(out, in_, idn, r, w):
        fw.op("pe", lambda e: e.transpose(out, in_, idn), r=r, w=w)

    def load_vecs(l):
        st = stage[0]
        rows = [(norm_ffn1[l], 8), (norm_mix[l], 8), (norm_ffn2[l], 8), (q_norm[l], 3), (kv_norm[l], 2),
                (ssm_d[l], 8), (ada_b[l], 72), (final_norm, 8)]
        r0 = 0
        for (v, n) in rows:
            fw.dma("sp", st[r0:r0 + n, 0:128], v.rearrange("(k p) -> k p", p=128), owner=st, w=[st])
            r0 += n
        p = psget()
        transpose(p[:, 0:r0], st[0:r0, 0:128], ident[0:r0, 0:r0], r=[st, ident], w=[p])
        fw.op("dve", lambda e: e.tensor_copy(vecT[l][:, 0:r0], p[:, 0:r0]), r=[p], w=[vecT[l]])
    V_N1, V_NM, V_N2, V_QN, V_KVN, V_SD, V_AB, V_FN = 0, 8, 16, 24, 27, 29, 37, 109

    def compute_mod(l):
        st = stage[1]
        fw.dma("sp", st[0:17, :], cvec, owner=st, w=[st])
        scT = fw_scT
        for k in range(KT):
            p = psget()
            transpose(p[:, 0:17], st[0:17, k * 128:(k + 1) * 128], ident[0:17, 0:17], r=[st, ident], w=[p])
            fw.op("act", lambda e: e.activation(out=scT[:, k, :], in_=p[:, 0:17], func=AF.Silu), r=[p], w=[scT])
        if cfg.get("mod_stage", 3) < 2:
            return
        c0 = len(ws_g.chunks)
        for q in range(18):
            ws_g.add(lambda t, q=q: [(vg(t), ada_w[l][:, q * 512:(q + 1) * 512].rearrange("(k p) n -> p k n", p=128))])
        for q in range(18):
            wtt = ws_g.get(c0 + q); wt = vg(wtt)
            p = psget()
            for j in range(4):
                for k in range(KT):
                    mm(p[:, j * 17:(j + 1) * 17], wt[:, k, j * 128:(j + 1) * 128], scT[:, k, :], k == 0, k == KT - 1,
                       r=[wtt, scT], w=[p])
            for j in range(4):
                ft = q * 4 + j
                fw.op("dve", lambda e: e.tensor_scalar(modT[l][:, ft, :], p[:, j * 17:(j + 1) * 17],
                                                      vecT[l][:, V_AB + ft:V_AB + ft + 1], None, ALU.add),
                      r=[p, vecT[l]], w=[modT[l]])
        if cfg.get("mod_stage", 3) < 3:
            return
        for i, (vn, jsh, jsc, jg, coef) in enumerate([(V_N1, 0, 1, 2, 0.5), (V_NM, 3, 4, 5, 1.0), (V_N2, 6, 7, 8, 0.5)]):
            fw.op("dve", lambda e: e.tensor_scalar(AB[l][:, 2 * i, :, :], modT[l][:, jsc * 8:(jsc + 1) * 8, :], 1.0, None, ALU.add),
                  r=[modT[l]], w=[AB[l]])
            fw.op("dve", lambda e: e.tensor_tensor(AB[l][:, 2 * i, :, :], AB[l][:, 2 * i, :, :],
                                                  vecT[l][:, vn:vn + 8].unsqueeze(2).broadcast_to([128, 8, 17]), ALU.mult),
                  r=[AB[l], vecT[l]], w=[AB[l]])
            fw.op("dve", lambda e: e.tensor_copy(AB[l][:, 2 * i + 1, :, :], modT[l][:, jsh * 8:(jsh + 1) * 8, :]),
                  r=[modT[l]], w=[AB[l]])
            fw.op("dve", lambda e: e.tensor_scalar(GT[l][:, i, :, :], modT[l][:, jg * 8:(jg + 1) * 8, :], coef, None, ALU.mult),
                  r=[modT[l]], w=[GT[l]])

    fw_scT = fw.sb([128, KT, 17], BF16, "scT")

    class Blk:
        pass

    def mkblk(bi):
        b = Blk()
        b.i = bi
        b.npr = BNPR[bi]
        b.has_s = (bi == 2)
        b.ncols = b.npr + (NSC if b.has_s else 0)
        b.groups = [(0, 512), (512, b.npr - 512)] + ([(b.npr, NSC)] if b.has_s else [])
        b.pgroups = [(0, 512), (512, b.npr - 512)]
        b.p0 = BSTART[bi]
        return b

    def load_x(b):
        ntt = b.npr // 128
        for g0 in range(0, ntt, 4):
            nt4 = min(4, ntt - g0)
            for tt in range(nt4):
                st = stage[tt % 2]
                t0 = b.p0 + (g0 + tt) * 128
                fw.dma("sp", st[:, :], xp[t0:t0 + 128, :], owner=st, w=[st])
                for k in range(KT):
                    p = PS[k]
                    transpose(p[:, tt * 128:(tt + 1) * 128], st[:, k * 128:(k + 1) * 128], ident[:], r=[st, ident], w=[p])
            for k in range(KT):
                if k % 2:
                    fw.op("act", lambda e: e.activation(out=xT[k][:, g0 * 128:(g0 + nt4) * 128], in_=PS[k][:, 0:nt4 * 128], func=AF.Copy), r=[PS[k]], w=[xT[k]])
                else:
                    fw.op("dve", lambda e: e.tensor_copy(xT[k][:, g0 * 128:(g0 + nt4) * 128], PS[k][:, 0:nt4 * 128]), r=[PS[k]], w=[xT[k]])
        if b.has_s:
            st = stage[0]
            fw.dma("sp", st[0:NSC, :], xs, owner=st, w=[st])
            for k in range(KT):
                p = PS[k]
                transpose(p[:, 0:NSC], st[0:NSC, k * 128:(k + 1) * 128], ident[0:NSC, 0:NSC], r=[st, ident], w=[p])
                fw.op("dve", lambda e: e.tensor_copy(xT[k][:, b.npr:b.npr + NSC], p[:, 0:NSC]), r=[p], w=[xT[k]])
        fw.dma("sp", rope_cs[:, :, 0:b.npr], rope_d[:, :, b.p0:b.p0 + b.npr].rearrange("c p n -> p c n"), owner=rope_cs, w=[rope_cs])
        if b.has_s:
            fw.dma("sp", rope_cs[:, :, b.npr:b.npr + NSC], rope_d[:, :, TP:TP + NSC].rearrange("c p n -> p c n"), owner=rope_cs, w=[rope_cs])

    def rms_stats(b, src, nk, dim):
        for (c0, n) in b.groups:
            p = psget()
            for k in range(nk):
                sq = sqb[k % 2]
                rows = src[k][1]
                fw.op("act", lambda e: e.activation(out=sq[0:rows, 0:n], in_=src[k][0][0:rows, c0:c0 + n], func=AF.Square),
                      r=[src[k][2]], w=[sq])
                mm(p[:, 0:n], onesb[0:rows, :], sq[0:rows, 0:n], k == 0, k == nk - 1, r=[sq, onesb], w=[p])
            fw.op("dve", lambda e: e.tensor_scalar(rstd[:, c0:c0 + n], p[:, 0:n], 1.0 / dim, EPS, ALU.mult, ALU.add), r=[p], w=[rstd])
            fw.op("act", lambda e: e.activation(out=rstd[:, c0:c0 + n], in_=rstd[:, c0:c0 + n], func=AF.Sqrt), r=[rstd], w=[rstd])
            fw.op("dve", lambda e: e.reciprocal(rstd[:, c0:c0 + n], rstd[:, c0:c0 + n]), r=[rstd], w=[rstd])

    def norm_mod(b, l, which, hT):
        rms_stats(b, [(xT[k], 128, xT[k]) for k in range(KT)], KT, D)
        A = AB[l][:, 2 * which, :, :]; Bb = AB[l][:, 2 * which + 1, :, :]
        for k in range(KT):
            for (c0, n) in b.groups:
                t = tmp()
                fw.op("dve", lambda e: e.tensor_tensor(t[:, 0:n], xT[k][:, c0:c0 + n], rstd[:, c0:c0 + n], ALU.mult),
                      r=[xT[k], rstd], w=[t])
                if c0 < b.npr:
                    fw.op("act", lambda e: e.activation(out=hT[k][:, c0:c0 + n], in_=t[:, 0:n], func=AF.Identity,
                                                        bias=Bb[:, k, 0:1], scale=A[:, k, 0:1]),
                          r=[t, AB[l]], w=[hT[k]])
                else:
                    t3 = t[:, 0:n].rearrange("p (s t) -> p s t", t=TS)
                    fw.op("dve", lambda e: e.tensor_tensor(t3, t3, A[:, k, 1:17].unsqueeze(2).broadcast_to([128, NSQ, TS]), ALU.mult),
                          r=[t, AB[l]], w=[t])
                    fw.op("dve", lambda e: e.tensor_tensor(hT[k][:, c0:c0 + n].rearrange("p (s t) -> p s t", t=TS), t3,
                                                          Bb[:, k, 1:17].unsqueeze(2).broadcast_to([128, NSQ, TS]), ALU.add),
                          r=[t, AB[l]], w=[hT[k]])

    def resid_add(b, l, gi, k, c0, n, p):
        Gt = GT[l][:, gi, :, :]
        if c0 < b.npr:
            fw.op("dve", lambda e: e.scalar_tensor_tensor(xT[k][:, c0:c0 + n], p[:, 0:n], Gt[:, k, 0:1], xT[k][:, c0:c0 + n],
                                                         ALU.mult, ALU.add), r=[p, GT[l], xT[k]], w=[xT[k]])
        else:
            t = tmp()
            t3 = t[:, 0:n].rearrange("p (s t) -> p s t", t=TS)
            fw.op("dve", lambda e: e.tensor_tensor(t3, p[:, 0:n].rearrange("p (s t) -> p s t", t=TS),
                                                  Gt[:, k, 1:17].unsqueeze(2).broadcast_to([128, NSQ, TS]), ALU.mult),
                  r=[p, GT[l]], w=[t])
            fw.op("dve", lambda e: e.tensor_tensor(xT[k][:, c0:c0 + n], xT[k][:, c0:c0 + n], t[:, 0:n], ALU.add),
                  r=[t, xT[k]], w=[xT[k]])

    def ffn(b, l, which, w_up, w_dn, gi):
        hT = slots[0:8]; hid = slots[8:30]
        norm_mod(b, l, which, hT)
        c0u = len(ws_up.chunks)
        for j in range(FT):
            ws_up.add(lambda t, j=j: [(vup(t)[:, :, 0:128], w_up[l][:, j * 128:(j + 1) * 128].rearrange("(k p) n -> p k n", p=128)),
                                      (vup(t)[:, :, 128:256], w_up[l][:, DFF + j * 128:DFF + (j + 1) * 128].rearrange("(k p) n -> p k n", p=128))])
        c0d = len(ws_dn.chunks)
        for m in range(KT):
            ws_dn.add(lambda t, m=m: [(vdn(t), w_dn[l][:, m * 128:(m + 1) * 128].rearrange("(j p) n -> p j n", p=128))])
        for j in range(FT):
            wtt = ws_up.get(c0u + j); wt = vup(wtt)
            for (c0, n) in b.groups:
                pa = psget(0, 4); pb = psget(0, 4)
                for k in range(KT):
                    mm(pa[:, 0:n], wt[:, k, 0:128], hT[k][:, c0:c0 + n], k == 0, k == KT - 1, r=[wtt, hT[k]], w=[pa], lazy=True)
                for k in range(KT):
                    mm(pb[:, 0:n], wt[:, k, 128:256], hT[k][:, c0:c0 + n], k == 0, k == KT - 1, r=[wtt, hT[k]], w=[pb], lazy=True)
                t = tmp()
                fw.op("act", lambda e: e.activation(out=t[:, 0:n], in_=pa[:, 0:n], func=AF.Silu), r=[pa], w=[t])
                fw.op("dve", lambda e: e.tensor_tensor(hid[j][:, c0:c0 + n], t[:, 0:n], pb[:, 0:n], ALU.mult), r=[t, pb], w=[hid[j]])
        for m in range(KT):
            wtt = ws_dn.get(c0d + m); wt = vdn(wtt)
            for (c0, n) in b.groups:
                p = psget(4, 8)
                for j in range(FT):
                    mm(p[:, 0:n], wt[:, j, :], hid[j][:, c0:c0 + n], j == 0, j == FT - 1, r=[wtt, hid[j]], w=[p], lazy=True)
                resid_add(b, l, gi, m, c0, n, p)

    def proj(b, hT, nk, wchunk, col0, ncol, evac):
        for mt in range((ncol + 127) // 128):
            mw = min(128, ncol - mt * 128)
            for (c0, n) in b.groups:
                p = psget(0, 4)
                for k in range(nk):
                    mm(p[0:mw, 0:n], wchunk[1][:, k, col0 + mt * 128:col0 + mt * 128 + mw], hT[k][:, c0:c0 + n], k == 0, k == nk - 1,
                       r=[wchunk[0], hT[k]], w=[p], lazy=True)
                evac(mt, mw, c0, n, p)

    def mixer(b, l):
        hT = slots[0:8]
        norm_mod(b, l, 1, hT)
        cq = slots[8:11]; mixed = slots[11:19]
        qn = slots[19:21]; qr = slots[21:23]; qlat = slots[23:27]; uT = slots[19:27]; oT = slots[27:29]; sga = slots[29]
        fw.dma("pool", wuv[:, :, :, :], w_uv[l].rearrange("(c p) h n -> p c h n", p=128), owner=wuv, w=[wuv])
        ci = ws.add(lambda t: [(vuk(t), w_uk[l].rearrange("(c p) h n -> p c h n", p=128))])
        wtt = ws.get(ci); wraw = vuk(wtt)
        for h in range(NH):
            for c in range(2):
                p = psget(0, 4)
                pb = p.t[:].bitcast(BF16)
                transpose(pb[:, 0:128], wraw[:, c, h, :], identb[:], r=[wtt, identb], w=[p])
                fw.op("dve", lambda e: e.tensor_copy(wukT[:, h, c * 128:(c + 1) * 128], pb[:, 0:128]), r=[p], w=[wukT])
        if cfg.get('ms', 9) < 2:
            return
        c0g = ws.add(lambda t: [(vg(t)[:, :, 0:384], w_in[l][:, OFF_CQ:OFF_CQ + 384].rearrange("(k p) n -> p k n", p=128))])
        ws.add(lambda t: [(vg(t)[:, :, 0:320], w_in[l][:, OFF_CKV:OFF_CKV + 320].rearrange("(k p) n -> p k n", p=128)),
                          (vg(t)[:, :, 320:352], w_in[l][:, OFF_KR + 32:OFF_KR + 64].rearrange("(k p) n -> p k n", p=128)),
                          (vg(t)[:, :, 352:384], w_in[l][:, OFF_KR:OFF_KR + 32].rearrange("(k p) n -> p k n", p=128))])
        wtt = ws.get(c0g); wt = vg(wtt)
        cqf = [fw_cqf[i] for i in range(3)]

        def ev_cq(mt, mw, c0, n, p):
            fw.op("act", lambda e: e.activation(out=cqf[mt][:, c0:c0 + n], in_=p[:, 0:n], func=AF.Copy), r=[p], w=[cqf[mt]])
        proj(b, hT, KT, (wtt, wt), 0, 384, ev_cq)
        rms_stats(b, [(cqf[i], 128, cqf[i]) for i in range(3)], 3, QL)
        for i in range(3):
            for (c0, n) in b.groups:
                fw.op("dve", lambda e: e.scalar_tensor_tensor(cq[i][:, c0:c0 + n], cqf[i][:, c0:c0 + n],
                                                             vecT[l][:, V_QN + i:V_QN + i + 1], rstd[:, c0:c0 + n], ALU.mult, ALU.mult),
                      r=[cqf[i], vecT[l], rstd], w=[cq[i]])
        if cfg.get('ms', 9) < 3:
            return
        wtt = ws.get(c0g + 1); wt = vg(wtt)
        fw.op("dve", lambda e: e.tensor_scalar(wt[:, :, 320:352], wt[:, :, 320:352], -1.0, None, ALU.mult), r=[wtt], w=[wtt])
        if cfg.get('ms', 9) < 3.2:
            return
        ckvf = [fw_cqf[0], fw_cqf[1]]; krf = fw_cqf[2]

        def ev_ckv(mt, mw, c0, n, p):
            fw.op("act", lambda e: e.activation(out=ckvf[mt][:, c0:c0 + n], in_=p[:, 0:n], func=AF.Copy), r=[p], w=[ckvf[mt]])
        proj(b, hT, KT, (wtt, wt), 0, 256, ev_ckv)
        if cfg.get('ms', 9) < 3.4:
            return
        rms_stats(b, [(ckvf[i], 128, ckvf[i]) for i in range(2)], 2, LAT)
        for i in range(2):
            for (c0, n) in b.groups:
                fw.op("dve", lambda e: e.scalar_tensor_tensor(ckvf[i][:, c0:c0 + n], ckvf[i][:, c0:c0 + n],
                                                             vecT[l][:, V_KVN + i:V_KVN + i + 1], rstd[:, c0:c0 + n], ALU.mult, ALU.mult),
                      r=[ckvf[i], vecT[l], rstd], w=[ckvf[i]])
                fw.op("act", lambda e: e.activation(out=kvT_L[:, i, c0:c0 + n], in_=ckvf[i][:, c0:c0 + n], func=AF.Copy), r=[ckvf[i]], w=[kvT_L])
        if cfg.get('ms', 9) < 3.6:
            return
        for (c0, n) in b.groups:
            p = psget(0, 4); p2 = psget(0, 4)
            for k in range(KT):
                mm(p[0:64, 0:n], wt[:, k, 256:320], hT[k][:, c0:c0 + n], k == 0, k == KT - 1, r=[wtt, hT[k]], w=[p])
            for k in range(KT):
                mm(p2[0:64, 0:n], wt[:, k, 320:384], hT[k][:, c0:c0 + n], k == 0, k == KT - 1, r=[wtt, hT[k]], w=[p2])
            if cfg.get('ms', 9) < 3.7:
                continue
            t = tmp()
            fw.op("dve", lambda e: e.tensor_tensor(krf[0:64, c0:c0 + n], p[0:64, 0:n], rope_cs[:, 0, c0:c0 + n], ALU.mult), r=[p, rope_cs], w=[krf])
            if cfg.get('ms', 9) < 3.8:
                continue
            fw.op("dve", lambda e: e.tensor_tensor(t[0:64, 0:n], p2[0:64, 0:n], rope_cs[:, 1, c0:c0 + n], ALU.mult), r=[p2, rope_cs], w=[t])
            fw.op("dve", lambda e: e.tensor_tensor(krf[0:64, c0:c0 + n], krf[0:64, c0:c0 + n], t[0:64, 0:n], ALU.add), r=[t, krf], w=[krf])
            fw.op("act", lambda e: e.activation(out=krT_L[:, c0:c0 + n], in_=krf[0:64, c0:c0 + n], func=AF.Copy), r=[krf], w=[krT_L])
        if cfg.get('ms', 9) < 4:
            return
        ntt = (b.ncols + 127) // 128
        for tt in range(ntt):
            t0 = tt * 128
            nt = min(128, b.ncols - t0)
            p = psget(4, 8)
            for i in range(2):
                transpose(p[0:nt, i * 128:(i + 1) * 128], ckvf[i][:, t0:t0 + nt], ident[:], r=[ckvf[i], ident], w=[p])
            if cfg.get('ms', 9) != 4.1:
                transpose(p[0:nt, 256:320], krf[0:64, t0:t0 + nt], ident[0:64, 0:64], r=[krf, ident], w=[p])
            st = stage[tt % 2]
            fw.op("dve", lambda e: e.tensor_copy(st[0:nt, 0:320], p[0:nt, 0:320]), r=[p], w=[st])
            fw.op("dve", lambda e: e.tensor_copy(vtok_L[0:nt, tt, :], st[0:nt, 0:256]), r=[st], w=[vtok_L])
            if t0 < b.npr:
                fw.dma("sp", ckv_p[l, b.p0 + t0:b.p0 + t0 + nt, :], st[0:nt, 0:256], owner=st, r=[st])
                fw.dma("sp", kr_p[l, b.p0 + t0:b.p0 + t0 + nt, :], st[0:nt, 256:320], owner=st, r=[st])
            else:
                fw.dma("sp", ckv_s[l, 0:nt, :], st[0:nt, 0:256], owner=st, r=[st])
                fw.dma("sp", kr_s[l, 0:nt, :], st[0:nt, 256:320], owner=st, r=[st])
        if b.i < 2:
            fw.op("dve", lambda e: e.tensor_copy(kvT_A[l][:, :, b.p0:b.p0 + b.npr], kvT_L[:, :, 0:b.npr]), r=[kvT_L], w=[kvT_A[l]])
            fw.op("act", lambda e: e.activation(out=krT_A[l][:, b.p0:b.p0 + b.npr], in_=krT_L[:, 0:b.npr], func=AF.Copy), r=[krT_L], w=[krT_A[l]])
            fw.op("dve", lambda e: e.tensor_copy(vtok_A[l][:, b.p0 // 128:(b.p0 + b.npr) // 128, :], vtok_L[:, 0:b.npr // 128, :]),
                  r=[vtok_L], w=[vtok_A[l]])
        if cfg.get('ms', 9) < 5:
            return
        c0ga = len(ws.chunks)
        kb0 = b.p0 // 128
        for h in range(NH):
            wq3 = w_uq[l].rearrange("(k p) n -> p k n", p=128)
            cqi = ws.add(lambda t, h=h: [(vq(t)[:, :, 0:192], wq3[:, :, h * 192:(h + 1) * 192]),
                                         (vq(t)[:, :, 192:224], wq3[:, :, h * 192 + 160:h * 192 + 192]),
                                         (vq(t)[:, :, 224:256], wq3[:, :, h * 192 + 128:h * 192 + 160])])
            wqt = ws.get(cqi, ahead=0); wq = vq(wqt)
            fw.op("dve", lambda e: e.tensor_scalar(wq[:, :, 192:224], wq[:, :, 192:224], -1.0, None, ALU.mult), r=[wqt], w=[wqt])
            for (c0, n) in b.groups:
                p = psget(0, 4)
                for k in range(3):
                    mm(p[:, 0:n], wq[:, k, 0:128], cq[k][:, c0:c0 + n], k == 0, k == 2, r=[wqt, cq[k]], w=[p])
                fw.op("act", lambda e: e.activation(out=qn[h % 2][:, c0:c0 + n], in_=p[:, 0:n], func=AF.Copy), r=[p], w=[qn[h % 2]])
                p = psget(0, 4); p2 = psget(0, 4)
                for k in range(3):
                    mm(p[0:64, 0:n], wq[:, k, 128:192], cq[k][:, c0:c0 + n], k == 0, k == 2, r=[wqt, cq[k]], w=[p])
                for k in range(3):
                    mm(p2[0:64, 0:n], wq[:, k, 192:256], cq[k][:, c0:c0 + n], k == 0, k == 2, r=[wqt, cq[k]], w=[p2])
                t = tmp(); t2 = tmp()
                fw.op("dve", lambda e: e.scalar_tensor_tensor(t[0:64, 0:n], p[0:64, 0:n], SCALE, rope_cs[:, 0, c0:c0 + n], ALU.mult, ALU.mult), r=[p, rope_cs], w=[t])
                fw.op("dve", lambda e: e.scalar_tensor_tensor(t2[0:64, 0:n], p2[0:64, 0:n], SCALE, rope_cs[:, 1, c0:c0 + n], ALU.mult, ALU.mult), r=[p2, rope_cs], w=[t2])
                fw.op("dve", lambda e: e.tensor_tensor(qr[h % 2][0:64, c0:c0 + n], t[0:64, 0:n], t2[0:64, 0:n], ALU.add),
                      r=[t, t2], w=[qr[h % 2]])
                for c in range(2):
                    p = psget(0, 4)
                    mm(p[:, 0:n], wukT[:, h, c * 128:(c + 1) * 128], qn[h % 2][:, c0:c0 + n], True, True, r=[wukT, qn[h % 2]], w=[p])
                    fw.op("act", lambda e: e.activation(out=qlat[(h % 2) * 2 + c][:, c0:c0 + n], in_=p[:, 0:n], func=AF.Copy, scale=SCALE),
                          r=[p], w=[qlat[(h % 2) * 2 + c]])
            ql = [qlat[(h % 2) * 2], qlat[(h % 2) * 2 + 1]]
            qrh = qr[h % 2]
            cga = ws.add(lambda t, h=h: [(vg(t)[:, :, 0:128], w_in[l][:, OFF_GA + h * 128:OFF_GA + (h + 1) * 128].rearrange("(k p) n -> p k n", p=128))])
            wgat = ws.get(cga, ahead=0); wga = vg(wgat)
            for (c0, n) in b.groups:
                p = psget(0, 4)
                for k in range(KT):
                    mm(p[:, 0:n], wga[:, k, 0:128], hT[k][:, c0:c0 + n], k == 0, k == KT - 1, r=[wgat, hT[k]], w=[p])
                fw.op("act", lambda e: e.activation(out=sga[:, c0:c0 + n], in_=p[:, 0:n], func=AF.Sigmoid), r=[p], w=[sga])
            if cfg.get('ms', 9) < 6:
                continue
            for (q0, nq) in b.pgroups:
                kq0 = kb0 + q0 // 128
                nkb = kq0 + nq // 128
                pO = [PS[4], PS[5]]; pD = PS[6]
                for kb in range(nkb):
                    if kb < kb0:
                        kT, krT, vt, kc, vi = kvT_A[l], krT_A[l], vtok_A[l], kb * 128, kb
                    else:
                        kk = kb - kb0
                        kT, krT, vt, kc, vi = kvT_L, krT_L, vtok_L, kk * 128, kk
                    i = kb - kq0
                    j0 = max(i, 0) * 128
                    nn = nq - j0
                    pS = psget(0, 4)
                    for c in range(2):
                        mm(pS[:, 0:nn], kT[:, c, kc:kc + 128], ql[c][:, q0 + j0:q0 + nq], c == 0, False, r=[kT, ql[c]], w=[pS])
                    mm(pS[:, 0:nn], krT[0:64, kc:kc + 128], qrh[0:64, q0 + j0:q0 + nq], False, True, r=[krT, qrh], w=[pS])
                    pt = fw_pt[kb % 2]
                    fw.op("act", lambda e: e.activation(out=pt[:, 0:nn], in_=pS[:, 0:nn], func=AF.Exp), r=[pS], w=[pt])
                    if i >= 0:
                        fw.op("dve", lambda e: e.tensor_tensor(pt[:, 0:128], pt[:, 0:128], trib[:], ALU.mult), r=[pt, trib], w=[pt])
                    for c in range(2):
                        mm(pO[c][:, j0:nq], vt[:, vi, c * 128:(c + 1) * 128], pt[:, 0:nn], kb == 0, kb == nkb - 1, r=[vt, pt], w=[pO[c]])
                    mm(pD[:, j0:nq], onesb[:], pt[:, 0:nn], kb == 0, kb == nkb - 1, r=[onesb, pt], w=[pD])
                rd = tmp()
                fw.op("dve", lambda e: e.reciprocal(rd[:, 0:nq], pD[:, 0:nq]), r=[pD], w=[rd])
                for c in range(2):
                    fw.op("dve", lambda e: e.tensor_tensor(oT[c][:, q0:q0 + nq], pO[c][:, 0:nq], rd[:, 0:nq], ALU.mult), r=[pO[c], rd], w=[oT[c]])
                p = PS[7]
                for c in range(2):
                    mm(p[:, 0:nq], wuv[:, c, h, :], oT[c][:, q0:q0 + nq], c == 0, c == 1, r=[wuv, oT[c]], w=[p])
                fw.op("dve", lambda e: e.tensor_tensor(mixed[h][:, q0:q0 + nq], p[:, 0:nq], sga[:, q0:q0 + nq], ALU.mult), r=[p, sga], w=[mixed[h]])
                if cfg.get('zero_pattn', False):
                    fw.op("dve", lambda e: e.memset(mixed[h][:, q0:q0 + nq], 0.0), w=[mixed[h]])
            if b.has_s:
                for c in range(2):
                    fw.op("dve", lambda e: e.tensor_copy(qsl[:, c, h, :], ql[c][:, b.npr:b.npr + NSC]), r=[ql[c]], w=[qsl])
                fw.op("dve", lambda e: e.tensor_copy(qsr[:, h, :], qrh[0:64, b.npr:b.npr + NSC]), r=[qrh], w=[qsr])
                fw.op("dve", lambda e: e.tensor_copy(sgas[:, h, :], sga[:, b.npr:b.npr + NSC]), r=[sga], w=[sgas])
        if b.has_s:
            sample_attn(b, l, mixed)
        if not cfg.get('zero_ssm', False):
            s5(b, l, hT, mixed)
        c0o = len(ws.chunks)
        for q in range(2):
            ws.add(lambda t, q=q: [(vg(t), w_out[l][:, q * 512:(q + 1) * 512].rearrange("(k p) n -> p k n", p=128))])
        for q in range(2):
            wtt = ws.get(c0o + q)

            def ev_out(mt, mw, c0, n, p, q=q):
                resid_add(b, l, 1, q * 4 + mt, c0, n, p)
            proj(b, mixed, KT, (wtt, vg(wtt)), 0, 512, ev_out)

    fw_cqf = [fw.sb([128, NCB], F32, "cqf%d" % i) for i in range(3)]
    qsl = fw.sb([128, 2, NH, NSC], BF16, "qsl"); qsr = fw.sb([64, NH, NSC], BF16, "qsr"); sgas = fw.sb([128, NH, NSC], BF16, "sgas")
    ptb = fw.sb([128, NSQ, 8], I32, "ptb"); idx = fw.sb([128, NSQ, 8], I32, "idx"); idx1 = fw.sb([128, NSQ, 8], I32, "idx1"); ecol = fw.sb([128, 1], F32, "ecol_s")
    Gc = [fw.sb([128, 8, 256], BF16, "Gc%d" % i) for i in range(2)]; Gk = [fw.sb([128, 8, 64], BF16, "Gk%d" % i) for i in range(2)]
    kTs = fw.sb([128, 3, 512], BF16, "kTs"); ptS = []
    ptn = fw.sb([64, 32], BF16, "ptn"); onb = fw.sb([32, 256], BF16, "onb"); oTs = fw.sb([128, 2, 32], BF16, "oTs")
    rds = fw.sb([32, 2], F32, "rds"); mnew = fw.sb([64, NSQ, 32], BF16, "mnew_s")
    fw_pt = [fw.sb([128, 512], BF16, "pt%d" % i) for i in range(2)]
    ptS.append(fw_pt[0])
    s5k = fw.sb([128, 102], F32, "s5k")
    prm = [fw.sb([128, 10, 32], F32, "prm%d" % l) for l in range(2)]
    gcar = [fw.sb([128, 32, 2], F32, "gcar%d" % l) for l in range(2)]
    hfin = fw.sb([128, 2, 32], F32, "hfin")
    Btl = fw.sb([128, 4, 4, 16], F32, "Btl")
    CTs = fw.sb([128, 2, 64], F32, "CTs")
    Cin = fw.sb([64, 2, 128], F32, "Cin")
    H0t = fw.sb([128, 2, 4, 16], F32, "H0t")
    H0in = Tile(Gc[0].t[:].rearrange("p r f -> p (r f)").bitcast(F32).rearrange("p (a x) -> p a x", a=2), "H0in"); H0in.b = Gc[0].b
    sst = Tile(Gc[1].t[:].rearrange("p r f -> p (r f)").bitcast(F32)[:, 0:384].rearrange("p (a x) -> p a x", a=6), "sst"); sst.b = Gc[1].b
    pcol = fw.sb([128, 8], F32, "pcol")

    def setup_idx():
        fw.dma("sp", ecol[:], ecol_d, owner=ecol, w=[ecol])
        fw.dma("pool", mnew[:], mnew_d, owner=mnew, w=[mnew])
        src = ptab.rearrange("s (qd j) -> j s qd", j=8)
        for e16 in range(16):
            fw.dma("sp", ptb[e16 * 8:(e16 + 1) * 8, :, :], src, owner=ptb, w=[ptb], allow_slow_non_contiguous=True)
        fw.op("dve", lambda e: e.tensor_scalar(idx[:], ptb[:], 16.0, ecol[:, 0:1], ALU.mult, ALU.add), r=[ptb, ecol], w=[idx])
        fw.op("dve", lambda e: e.tensor_scalar(idx1[:], idx[:], float(nphys * 16), None, ALU.add), r=[idx], w=[idx1])

    def sample_attn(b, l, mixed):
        SA = cfg.get('sa', 9)
        if SA < 1:
            for h in range(NH):
                fw.op('dve', lambda e: e.memset(mixed[h][:, b.npr:b.npr + NSC], 0.0), w=[mixed[h]])
            return
        rows_c = cache_ckv_l[l].rearrange("n (e r) f -> (n e) (r f)", e=16)
        rows_k = cache_kr_l[l].rearrange("n (e r) f -> (n e) (r f)", e=16)
        idxl = idx
        npr = b.npr
        pY = PS[6]; pO = PS[5]; pN = PS[7]
        pT = [PS[2], PS[3], PS[4]]
        pTb = [x.t[:].bitcast(BF16) for x in pT]
        vt_new = vtok_L[0:64, npr // 128, :]
        step = [0]
        for sq in range(NSQ):
            qc = [qsl[:, c, :, sq * 4:(sq + 1) * 4] for c in range(2)]
            qrp = qsr[:, :, sq * 4:(sq + 1) * 4]
            for qd in range(8):
                gi = step[0] % 2
                step[0] += 1
                gc, gk = Gc[gi], Gk[gi]
                fw._wait("pool", [idxl.b], [gc.b, gk.b])
                for (g_t, rows, oap) in ((gc, rows_c, gc[:, :, :].rearrange("p r f -> p (r f)")), (gk, rows_k, gk[:, :, :].rearrange("p r f -> p (r f)"))):
                    if g_t.b.dsem is None:
                        g_t.b.dsem = es.enter_context(nc.semaphore("d%d" % len(fw.dsems)))
                        fw.dsems.append(g_t.b.dsem)
                        fw.dtot[id(g_t.b.dsem)] = (g_t.b.dsem, 0)
                    sm = g_t.b.dsem
                    tot = fw.dtot[id(sm)][1] + 16
                    fw.dtot[id(sm)] = (sm, tot)
                    nc.gpsimd.indirect_dma_start(out=oap, out_offset=None, in_=rows,
                                                 in_offset=bass.IndirectOffsetOnAxis(ap=idxl[:, sq, qd:qd + 1], axis=0)).then_inc(sm, 16)
                    fw._rec((sm, tot), [idxl.b], [g_t.b])
                    fw.nins += 1
                if SA < 2:
                    continue
                pS = PS[step[0] % 2]
                pt = ptS[0]
                for rg in range(2):
                    for rr in range(4):
                        r = rg * 4 + rr
                        transpose(pTb[0][:, rr * 128:(rr + 1) * 128], gc[:, r, 0:128], identb[:], r=[gc, identb], w=[pT[0]])
                        transpose(pTb[1][:, rr * 128:(rr + 1) * 128], gc[:, r, 128:256], identb[:], r=[gc, identb], w=[pT[1]])
                        transpose(pTb[2][0:64, rr * 128:(rr + 1) * 128], gk[:, r, :], identb[:], r=[gk, identb], w=[pT[2]])
                    fw.op("act", lambda e: e.activation(out=kTs[:, 0, :], in_=pTb[0][:, 0:512], func=AF.Copy), r=[pT[0]], w=[kTs])
                    fw.op("dve", lambda e: e.tensor_copy(kTs[:, 1, :], pTb[1][:, 0:512]), r=[pT[1]], w=[kTs])
                    fw.op("act", lambda e: e.activation(out=kTs[0:64, 2, :], in_=pTb[2][0:64, 0:512], func=AF.Copy), r=[pT[2]], w=[kTs])
                    if SA < 3:
                        continue
                    for rr in range(4):
                        blk = rg * 4 + rr
                        o = pS[:, blk * 32:(blk + 1) * 32]
                        mm(o, kTs[:, 0, rr * 128:(rr + 1) * 128], qc[0], True, False, r=[kTs, qsl], w=[pS])
                        mm(o, kTs[:, 1, rr * 128:(rr + 1) * 128], qc[1], False, False, r=[kTs, qsl], w=[pS])
                        mm(o, kTs[0:64, 2, rr * 128:(rr + 1) * 128], qrp, False, True, r=[kTs, qsr], w=[pS])
                if SA < 4:
                    continue
                fw.op("act", lambda e: e.activation(out=pt[:, 0:256], in_=pS[:, 0:256], func=AF.Exp), r=[pS], w=[pt])
                for blk in range(8):
                    first = (qd == 0 and blk == 0)
                    mm(pO[0:32, 0:256], pt[:, blk * 32:(blk + 1) * 32], gc[:, blk, :], first, False, r=[pt, gc], w=[pO])
                    mm(pO[0:32, 256:258], pt[:, blk * 32:(blk + 1) * 32], onesb[:, 0:2], first, False, r=[pt, onesb], w=[pO])
            if SA < 5:
                continue
            for c in range(2):
                mm(pN[0:64, 0:32], kvT_L[:, c, npr:npr + NSC], qc[c], c == 0, False, r=[kvT_L, qsl], w=[pN])
            mm(pN[0:64, 0:32], krT_L[0:64, npr:npr + NSC], qrp, False, True, r=[krT_L, qsr], w=[pN])
            fw.op("act", lambda e: e.activation(out=ptn[:, :], in_=pN[0:64, 0:32], func=AF.Exp), r=[pN], w=[ptn])
            fw.op("dve", lambda e: e.tensor_tensor(ptn[:, :], ptn[:, :], mnew[:, sq, :], ALU.mult), r=[ptn, mnew], w=[ptn])
            mm(pO[0:32, 0:256], ptn[:, :], vt_new, False, True, r=[ptn, vtok_L], w=[pO])
            mm(pO[0:32, 256:258], ptn[:, :], onesb[0:64, 0:2], False, True, r=[ptn, onesb], w=[pO])
            fw.op("dve", lambda e: e.reciprocal(rds[:, 0:1], pO[0:32, 256:257]), r=[pO], w=[rds])
            fw.op("dve", lambda e: e.tensor_scalar(onb[:, :], pO[0:32, 0:256], rds[:, 0:1], None, ALU.mult), r=[pO, rds], w=[onb])
            for c in range(2):
                transpose(pTb[0][:, c * 32:(c + 1) * 32], onb[:, c * 128:(c + 1) * 128], identb[0:32, 0:32], r=[onb, identb], w=[pT[0]])
            fw.op("dve", lambda e: e.tensor_copy(oTs[:, :, :], pTb[0][:, 0:64].rearrange("p (c x) -> p c x", c=2)), r=[pT[0]], w=[oTs])
            for h in range(NH):
                for c in range(2):
                    mm(pY[:, (sq * 8 + h) * 4:(sq * 8 + h) * 4 + 4], wuv[:, c, h, :], oTs[:, c, h * 4:(h + 1) * 4], c == 0, c == 1,
                       r=[wuv, oTs], w=[pY])
        if SA < 5:
            for h in range(NH):
                fw.op('dve', lambda e: e.memset(mixed[h][:, b.npr:b.npr + NSC], 0.0), w=[mixed[h]])
            return
        pY4 = pY[:, :].rearrange("p (s h t) -> p s h t", s=NSQ, h=NH)
        for h in range(NH):
            fw.op("dve", lambda e: e.tensor_tensor(mixed[h][:, npr:npr + NSC].rearrange("p (s t) -> p s t", t=TS), pY4[:, :, h, :],
                                                  sgas[:, h, :].rearrange("p (s t) -> p s t", t=TS), ALU.mult), r=[pY, sgas], w=[mixed[h]])

    TWO_PI = 2.0 * math.pi
    K_MB, K_RM, K_MQ, K_SM = 0, 2, 6, 38

    def dv(fn, r, w):
        fw.op("dve", fn, r=r, w=w)

    def s5_params(l):
        P = prm[l]
        fw.dma("sp", P[:, 0, :], a_re[l].rearrange("(q g2) p -> (g2 p) q", g2=2), owner=P, w=[P], allow_slow_non_contiguous=True)
        fw.dma("sp", P[:, 1, :], a_im[l].rearrange("(q g2) p -> (g2 p) q", g2=2), owner=P, w=[P], allow_slow_non_contiguous=True)
        st = stage[0]
        fw.dma("sp", st[0:1, 0:64], log_dt[l].rearrange("(a g) -> a g", a=1), owner=st, w=[st])
        p = psget()
        fw.op("pe", lambda e: e.matmul(p[:, 0:64], ident[0:1, :].bitcast(F32) if False else onesf[0:1, :], st[0:1, 0:64], start=True, stop=True), r=[st, onesf], w=[p])
        pv = p[:, 0:64].rearrange("p (q g2) -> p q g2", g2=2)
        dv(lambda e: e.tensor_copy(P[0:64, 2, :], pv[0:64, :, 0]), [p], [P])
        dv(lambda e: e.tensor_copy(P[64:128, 2, :], pv[64:128, :, 1]), [p], [P])
        fw.op("act", lambda e: e.activation(out=P[:, 2, :], in_=P[:, 2, :], func=AF.Exp), r=[P], w=[P])
        dv(lambda e: e.tensor_tensor(P[:, 3, :], P[:, 0, :], P[:, 2, :], ALU.mult), [P], [P])
        dv(lambda e: e.tensor_tensor(P[:, 4, :], P[:, 1, :], P[:, 2, :], ALU.mult), [P], [P])
        fw.op("act", lambda e: e.activation(out=P[:, 3, :], in_=P[:, 3, :], func=AF.Exp), r=[P], w=[P])
        ti = tmp(); tiv = ti[:, 0:32].bitcast(I32)
        dv(lambda e: e.tensor_scalar(tiv, P[:, 4, :], 1.0 / TWO_PI, None, ALU.mult), [P], [ti])
        dv(lambda e: e.scalar_tensor_tensor(P[:, 9, :], tiv, -TWO_PI, P[:, 4, :], ALU.mult, ALU.add), [ti, P], [P])
        fw.op("act", lambda e: e.activation(out=P[:, 6, :], in_=P[:, 9, :], func=AF.Sin), r=[P], w=[P])
        dv(lambda e: e.tensor_scalar(tiv, P[:, 4, :], 1.0 / TWO_PI, 0.25, ALU.mult, ALU.add), [P], [ti])
        dv(lambda e: e.scalar_tensor_tensor(P[:, 9, :], tiv, -TWO_PI, P[:, 4, :], ALU.mult, ALU.add), [ti, P], [P])
        fw.op("act", lambda e: e.activation(out=P[:, 5, :], in_=P[:, 9, :], func=AF.Sin, bias=npi[:, 0:1]), r=[P, npi], w=[P])
        dv(lambda e: e.tensor_tensor(P[:, 5, :], P[:, 5, :], P[:, 3, :], ALU.mult), [P], [P])
        dv(lambda e: e.tensor_tensor(P[:, 6, :], P[:, 6, :], P[:, 3, :], ALU.mult), [P], [P])
        t = tmp()
        dv(lambda e: e.tensor_tensor(t[:, 0:32], P[:, 0, :], P[:, 0, :], ALU.mult), [P], [t])
        dv(lambda e: e.tensor_tensor(t[:, 32:64], P[:, 1, :], P[:, 1, :], ALU.mult), [P], [t])
        dv(lambda e: e.tensor_tensor(t[:, 0:32], t[:, 0:32], t[:, 32:64], ALU.add), [t], [t])
        dv(lambda e: e.reciprocal(t[:, 0:32], t[:, 0:32]), [t], [t])
        dv(lambda e: e.tensor_scalar(t[:, 64:96], P[:, 5, :], -1.0, None, ALU.add), [P], [t])
        dv(lambda e: e.tensor_tensor(t[:, 96:128], t[:, 64:96], P[:, 0, :], ALU.mult), [t, P], [t])
        dv(lambda e: e.tensor_tensor(t[:, 128:160], P[:, 6, :], P[:, 1, :], ALU.mult), [t, P], [t])
        dv(lambda e: e.tensor_tensor(t[:, 96:128], t[:, 96:128], t[:, 128:160], ALU.add), [t], [t])
        dv(lambda e: e.tensor_tensor(P[:, 7, :], t[:, 96:128], t[:, 0:32], ALU.mult), [t], [P])
        dv(lambda e: e.tensor_tensor(t[:, 96:128], P[:, 6, :], P[:, 0, :], ALU.mult), [t, P], [t])
        dv(lambda e: e.tensor_tensor(t[:, 128:160], t[:, 64:96], P[:, 1, :], ALU.mult), [t, P], [t])
        dv(lambda e: e.tensor_tensor(t[:, 96:128], t[:, 96:128], t[:, 128:160], ALU.subtract), [t], [t])
        dv(lambda e: e.tensor_tensor(P[:, 8, :], t[:, 96:128], t[:, 0:32], ALU.mult), [t], [P])
        dv(lambda e: e.memset(gcar[l][:], 0.0), [], [gcar[l]])

    def s5(b, l, hT, mixed):
        P = prm[l]
        npr = b.npr
        Sn, Cn, X, Y = fw_cqf[0], fw_cqf[1], fw_cqf[2], stage[0]
        iot = stage[1]
        KI = Tile(Gc[0].t[:].rearrange("p r f -> p (r f)").bitcast(I32), "KI"); KI.b = Gc[0].b
        fw.dma("sp", iot[:, 0:768], s5c_d[:, 0:768], owner=iot, w=[iot])
        uT = slots[19]; hS = slots[20]
        wbz = [slots[21], slots[22]]; wcz = [slots[23], slots[24]]
        zT = [slots[8], slots[9], slots[10], slots[25], slots[26], slots[27], slots[28], slots[29]]
        pgroups = b.pgroups
        for ft in range(KT):
            fw.dma("sp", Btl[:, 0, :, :], b_re[l][8 * ft:8 * ft + 8].rearrange("(q g2) p c -> (g2 p) q c", g2=2), owner=Btl, w=[Btl])
            fw.dma("sp", Btl[:, 1, :, :], b_im[l][8 * ft:8 * ft + 8].rearrange("(q g2) p c -> (g2 p) q c", g2=2), owner=Btl, w=[Btl])
            cr = P[:, 7, 4 * ft:4 * ft + 4].unsqueeze(2).broadcast_to([128, 4, 16])
            ci = P[:, 8, 4 * ft:4 * ft + 4].unsqueeze(2).broadcast_to([128, 4, 16])
            t = tmp()
            t4 = t[:, 0:64].rearrange("p (q c) -> p q c", c=16); t5 = t[:, 64:128].rearrange("p (q c) -> p q c", c=16)
            dv(lambda e: e.tensor_tensor(Btl[:, 2, :, :], Btl[:, 0, :, :], cr, ALU.mult), [Btl, P], [Btl])
            dv(lambda e: e.tensor_tensor(t4, Btl[:, 1, :, :], ci, ALU.mult), [Btl, P], [t])
            dv(lambda e: e.tensor_tensor(Btl[:, 2, :, :], Btl[:, 2, :, :], t4, ALU.subtract), [Btl, t], [Btl])
            dv(lambda e: e.tensor_tensor(Btl[:, 3, :, :], Btl[:, 1, :, :], cr, ALU.mult), [Btl, P], [Btl])
            dv(lambda e: e.tensor_tensor(t5, Btl[:, 0, :, :], ci, ALU.mult), [Btl, P], [t])
            dv(lambda e: e.tensor_tensor(Btl[:, 3, :, :], Btl[:, 3, :, :], t5, ALU.add), [Btl, t], [Btl])
            mbd = s5k[:, K_MB:K_MB + 2].unsqueeze(1).unsqueeze(3).broadcast_to([128, 4, 2, 16])
            for ri in range(2):
                xe = tmp()
                xe4 = xe[:, 0:128].rearrange("p (q g c) -> p q g c", q=4, g=2)
                dv(lambda e: e.tensor_tensor(xe4, Btl[:, 2 + ri, :, :].unsqueeze(2).broadcast_to([128, 4, 2, 16]), mbd, ALU.mult), [Btl, s5k], [xe])
                p = psget(6, 8)
                transpose(p[:, 0:128], xe[:, 0:128], ident[:], r=[xe, ident], w=[p])
                for ql in range(4):
                    dv(lambda e: e.tensor_scalar(wbz[ri][:, ql * 128:(ql + 1) * 128], p[:, 0:128], s5k[:, K_RM + ql:K_RM + ql + 1], None, ALU.mult),
                       [p, s5k], [wbz[ri]])
            for ri, csrc in enumerate((c_re, c_im)):
                for ql in range(4):
                    fw.dma("sp", Cin[16 * ql:16 * ql + 16, ri, :].rearrange("c (g p) -> c g p", g=2),
                           csrc[l][8 * ft + 2 * ql:8 * ft + 2 * ql + 2].rearrange("g c p -> c g p"), owner=Cin, w=[Cin])
            for ri in range(2):
                p = psget(6, 8)
                transpose(p[:, 0:64], Cin[0:64, ri, :], ident[0:64, 0:64], r=[Cin, ident], w=[p])
                dv(lambda e: e.tensor_copy(CTs[:, ri, :], p[:, 0:64]), [p], [CTs])
                for ql in range(4):
                    mq = s5k[:, K_MQ + ql * 8:K_MQ + ql * 8 + 8].rearrange("p (a g) -> p a g", g=2).unsqueeze(3).broadcast_to([128, 4, 2, 16])
                    src = CTs[:, ri, ql * 16:(ql + 1) * 16].unsqueeze(1).unsqueeze(1).broadcast_to([128, 4, 2, 16])
                    o4 = wcz[ri][:, ql * 128:(ql + 1) * 128].rearrange("p (a g c) -> p a g c", a=4, g=2)
                    dv(lambda e: e.tensor_tensor(o4, src, mq, ALU.mult), [CTs, s5k], [wcz[ri]])
                    if ri == 1:
                        dv(lambda e: e.tensor_scalar(wcz[ri][:, ql * 128:(ql + 1) * 128], wcz[ri][:, ql * 128:(ql + 1) * 128], -1.0, None, ALU.mult), [wcz[ri]], [wcz[ri]])
            cu = ws.add(lambda t_, ft=ft: [(vg(t_)[:, :, 0:128], w_in[l][:, OFF_U + ft * 128:OFF_U + (ft + 1) * 128].rearrange("(k p) n -> p k n", p=128))])
            wut = ws.get(cu, ahead=0); wu = vg(wut)
            for (c0, n) in b.groups:
                p = psget(6, 8)
                for k in range(KT):
                    mm(p[:, 0:n], wu[:, k, 0:128], hT[k][:, c0:c0 + n], k == 0, k == KT - 1, r=[wut, hT[k]], w=[p])
                fw.op("act", lambda e: e.activation(out=uT[:, c0:c0 + n], in_=p[:, 0:n], func=AF.Copy), r=[p], w=[uT])
            if b.has_s:
                for ri, ssrc in enumerate((st_re, st_im)):
                    fw.dma("sp", H0in[0:16, ri, :], ssrc[l][:, 8 * ft:8 * ft + 8, :].rearrange("s g p -> s (g p)"), owner=H0in, w=[H0in])
                    for ql in range(4):
                        p = psget(6, 8)
                        transpose(p[:, 0:16], H0in[0:16, ri, ql * 128:(ql + 1) * 128], ident[0:16, 0:16], r=[H0in, ident], w=[p])
                        dv(lambda e: e.tensor_copy(H0t[:, ri, ql, :], p[:, 0:16]), [p], [H0t])
            pYs = [PS[4], PS[5]]
            pYsm = PS[6]
            for ql in range(4):
                q = 4 * ft + ql
                th = P[:, 4, q:q + 1]; rr_ = P[:, 3, q:q + 1]
                dv(lambda e: e.tensor_scalar(pcol[:, 0:1], th, float(b.p0), TWO_PI, ALU.mult, ALU.add), [P], [pcol])
                pB = [[PS[0], PS[1]], [PS[2], PS[3]]]
                for gi, (c0, n) in enumerate(pgroups):
                    for ri in range(2):
                        mm(pB[gi][ri][:, 0:n], wbz[ri][:, ql * 128:(ql + 1) * 128], uT[:, c0:c0 + n], True, True, r=[wbz[ri], uT], w=[pB[gi][ri]])
                fw.op("act", lambda e: e.activation(out=X[:, 0:npr], in_=iot[:, 0:npr], func=AF.Identity, scale=th, bias=pcol[:, 0:1]),
                      r=[iot, P, pcol], w=[X])
                dv(lambda e: e.tensor_scalar(KI[:, 0:npr], X[:, 0:npr], 1.0 / TWO_PI, None, ALU.mult), [X], [KI])
                dv(lambda e: e.scalar_tensor_tensor(Y[:, 0:npr], KI[:, 0:npr], -TWO_PI, X[:, 0:npr], ALU.mult, ALU.add), [KI, X], [Y])
                fw.op("act", lambda e: e.activation(out=Sn[:, 0:npr], in_=Y[:, 0:npr], func=AF.Sin), r=[Y], w=[Sn])
                dv(lambda e: e.tensor_scalar(KI[:, 0:npr], X[:, 0:npr], 1.0 / TWO_PI, 0.25, ALU.mult, ALU.add), [X], [KI])
                dv(lambda e: e.scalar_tensor_tensor(Y[:, 0:npr], KI[:, 0:npr], -TWO_PI, X[:, 0:npr], ALU.mult, ALU.add), [KI, X], [Y])
                fw.op("act", lambda e: e.activation(out=Cn[:, 0:npr], in_=Y[:, 0:npr], func=AF.Sin, bias=npi[:, 0:1]), r=[Y, npi], w=[Cn])
                for gi, (c0, n) in enumerate(pgroups):
                    br, bi = pB[gi][0], pB[gi][1]
                    t = tmp()
                    dv(lambda e: e.tensor_tensor(X[:, c0:c0 + n], br[:, 0:n], Cn[:, c0:c0 + n], ALU.mult), [br, Cn], [X])
                    dv(lambda e: e.tensor_tensor(t[:, 0:n], bi[:, 0:n], Sn[:, c0:c0 + n], ALU.mult), [bi, Sn], [t])
                    dv(lambda e: e.tensor_tensor(X[:, c0:c0 + n], X[:, c0:c0 + n], t[:, 0:n], ALU.add), [X, t], [X])
                    dv(lambda e: e.tensor_tensor(Y[:, c0:c0 + n], bi[:, 0:n], Cn[:, c0:c0 + n], ALU.mult), [bi, Cn], [Y])
                    dv(lambda e: e.tensor_tensor(t[:, 0:n], br[:, 0:n], Sn[:, c0:c0 + n], ALU.mult), [br, Sn], [t])
                    dv(lambda e: e.tensor_tensor(Y[:, c0:c0 + n], Y[:, c0:c0 + n], t[:, 0:n], ALU.subtract), [Y, t], [Y])
                rb = rr_.broadcast_to([128, npr])
                dv(lambda e: e.tensor_tensor_scan(X[:, 0:npr], rb, X[:, 0:npr], gcar[l][:, q, 0:1], ALU.mult, ALU.add), [X, P, gcar[l]], [X])
                dv(lambda e: e.tensor_tensor_scan(Y[:, 0:npr], rb, Y[:, 0:npr], gcar[l][:, q, 1:2], ALU.mult, ALU.add), [Y, P, gcar[l]], [Y])
                dv(lambda e: e.tensor_copy(gcar[l][:, q, 0:1], X[:, npr - 1:npr]), [X], [gcar[l]])
                dv(lambda e: e.tensor_copy(gcar[l][:, q, 1:2], Y[:, npr - 1:npr]), [Y], [gcar[l]])
                if b.i == 2:
                    t = tmp()
                    e0 = npr - 1
                    dv(lambda e: e.tensor_tensor(t[:, 0:1], Cn[:, e0:e0 + 1], X[:, e0:e0 + 1], ALU.mult), [Cn, X], [t])
                    dv(lambda e: e.tensor_tensor(t[:, 1:2], Sn[:, e0:e0 + 1], Y[:, e0:e0 + 1], ALU.mult), [Sn, Y], [t])
                    dv(lambda e: e.tensor_tensor(hfin[:, 0, q:q + 1], t[:, 0:1], t[:, 1:2], ALU.subtract), [t], [hfin])
                    dv(lambda e: e.tensor_tensor(t[:, 2:3], Sn[:, e0:e0 + 1], X[:, e0:e0 + 1], ALU.mult), [Sn, X], [t])
                    dv(lambda e: e.tensor_tensor(t[:, 3:4], Cn[:, e0:e0 + 1], Y[:, e0:e0 + 1], ALU.mult), [Cn, Y], [t])
                    dv(lambda e: e.tensor_tensor(hfin[:, 1, q:q + 1], t[:, 2:3], t[:, 3:4], ALU.add), [t], [hfin])
                for ri in range(2):
                    for gi, (c0, n) in enumerate(pgroups):
                        t = tmp(); t2 = tmp()
                        a_, b_ = (Cn, Sn) if ri == 0 else (Sn, Cn)
                        dv(lambda e: e.tensor_tensor(t[:, 0:n], a_[:, c0:c0 + n], X[:, c0:c0 + n], ALU.mult), [a_, X], [t])
                        dv(lambda e: e.tensor_tensor(t2[:, 0:n], b_[:, c0:c0 + n], Y[:, c0:c0 + n], ALU.mult), [b_, Y], [t2])
                        dv(lambda e: e.tensor_tensor(hS[:, c0:c0 + n], t[:, 0:n], t2[:, 0:n], ALU.subtract if ri == 0 else ALU.add), [t, t2], [hS])
                        mm(pYs[gi][:, 0:n], wcz[ri][:, ql * 128:(ql + 1) * 128], hS[:, c0:c0 + n], (ql == 0 and ri == 0), (ql == 3 and ri == 1),
                           r=[wcz[ri], hS], w=[pYs[gi]])
                if b.has_s:
                    c0 = npr
                    pBs = [PS[0], PS[1]]
                    for ri in range(2):
                        mm(pBs[ri][:, 0:NSC], wbz[ri][:, ql * 128:(ql + 1) * 128], uT[:, c0:c0 + NSC], True, True, r=[wbz[ri], uT], w=[pBs[ri]])
                    S = sst
                    lbr = P[:, 5, q:q + 1]; lbi = P[:, 6, q:q + 1]
                    dv(lambda e: e.tensor_copy(S[:, 0, :], pBs[0][:, 0:NSC]), [pBs[0]], [S])
                    dv(lambda e: e.tensor_copy(S[:, 1, :], pBs[1][:, 0:NSC]), [pBs[1]], [S])
                    s0r = S[:, 0, :].rearrange("p (s t) -> p s t", t=TS)[:, :, 0]
                    s0i = S[:, 1, :].rearrange("p (s t) -> p s t", t=TS)[:, :, 0]
                    h0r = H0t[:, 0, ql, :]; h0i = H0t[:, 1, ql, :]
                    dv(lambda e: e.scalar_tensor_tensor(s0r, h0r, lbr, s0r, ALU.mult, ALU.add), [H0t, P, S], [S])
                    dv(lambda e: e.tensor_scalar(pcol[:, 2:3], lbi, -1.0, None, ALU.mult), [P], [pcol])
                    dv(lambda e: e.scalar_tensor_tensor(s0r, h0i, pcol[:, 2:3], s0r, ALU.mult, ALU.add), [H0t, pcol, S], [S])
                    dv(lambda e: e.scalar_tensor_tensor(s0i, h0r, lbi, s0i, ALU.mult, ALU.add), [H0t, P, S], [S])
                    dv(lambda e: e.scalar_tensor_tensor(s0i, h0i, lbr, s0i, ALU.mult, ALU.add), [H0t, P, S], [S])
                    dv(lambda e: e.tensor_scalar(pcol[:, 3:4], th, 0.0, TWO_PI, ALU.mult, ALU.add), [P], [pcol])
                    fw.op("act", lambda e: e.activation(out=S[:, 2, :].rearrange("p (s t) -> p s t", t=TS),
                                                        in_=iot[:, 0:TS].unsqueeze(1).broadcast_to([128, NSQ, TS]), func=AF.Identity, scale=th, bias=pcol[:, 3:4]),
                          r=[iot, P, pcol], w=[S])
                    tk = tmp(); tkv = tk[:, 0:NSC].bitcast(I32)
                    dv(lambda e: e.tensor_scalar(tkv, S[:, 2, :], 1.0 / TWO_PI, None, ALU.mult), [S], [tk])
                    dv(lambda e: e.scalar_tensor_tensor(S[:, 3, :], tkv, -TWO_PI, S[:, 2, :], ALU.mult, ALU.add), [tk, S], [S])
                    fw.op("act", lambda e: e.activation(out=S[:, 4, :], in_=S[:, 3, :], func=AF.Sin), r=[S], w=[S])
                    dv(lambda e: e.tensor_scalar(tkv, S[:, 2, :], 1.0 / TWO_PI, 0.25, ALU.mult, ALU.add), [S], [tk])
                    dv(lambda e: e.scalar_tensor_tensor(S[:, 3, :], tkv, -TWO_PI, S[:, 2, :], ALU.mult, ALU.add), [tk, S], [S])
                    fw.op("act", lambda e: e.activation(out=S[:, 5, :], in_=S[:, 3, :], func=AF.Sin, bias=npi[:, 0:1]), r=[S, npi], w=[S])
                    t = tmp()
                    dv(lambda e: e.tensor_tensor(t[:, 0:64], S[:, 0, :], S[:, 5, :], ALU.mult), [S], [t])
                    dv(lambda e: e.tensor_tensor(t[:, 128:192], S[:, 1, :], S[:, 4, :], ALU.mult), [S], [t])
                    dv(lambda e: e.tensor_tensor(t[:, 0:64], t[:, 0:64], t[:, 128:192], ALU.add), [t], [t])
                    dv(lambda e: e.tensor_tensor(t[:, 64:128], S[:, 1, :], S[:, 5, :], ALU.mult), [S], [t])
                    dv(lambda e: e.tensor_tensor(t[:, 128:192], S[:, 0, :], S[:, 4, :], ALU.mult), [S], [t])
                    dv(lambda e: e.tensor_tensor(t[:, 64:128], t[:, 64:128], t[:, 128:192], ALU.subtract), [t], [t])
                    dv(lambda e: e.tensor_scalar(t[:, 192:256], s5k[:, K_SM:K_SM + 64], rr_, None, ALU.mult), [s5k, P], [t])
                    dv(lambda e: e.tensor_tensor_scan(t[:, 0:64], t[:, 192:256], t[:, 0:64], 0.0, ALU.mult, ALU.add), [t], [t])
                    dv(lambda e: e.tensor_tensor_scan(t[:, 64:128], t[:, 192:256], t[:, 64:128], 0.0, ALU.mult, ALU.add), [t], [t])
                    dv(lambda e: e.tensor_tensor(S[:, 0, :], S[:, 5, :], t[:, 0:64], ALU.mult), [S, t], [S])
                    dv(lambda e: e.tensor_tensor(S[:, 2, :], S[:, 4, :], t[:, 64:128], ALU.mult), [S, t], [S])
                    dv(lambda e: e.tensor_tensor(S[:, 0, :], S[:, 0, :], S[:, 2, :], ALU.subtract), [S], [S])
                    dv(lambda e: e.tensor_tensor(S[:, 1, :], S[:, 4, :], t[:, 0:64], ALU.mult), [S, t], [S])
                    dv(lambda e: e.tensor_tensor(S[:, 2, :], S[:, 5, :], t[:, 64:128], ALU.mult), [S, t], [S])
                    dv(lambda e: e.tensor_tensor(S[:, 1, :], S[:, 1, :], S[:, 2, :], ALU.add), [S], [S])
                    for ri in range(2):
                        dv(lambda e: e.tensor_copy(hS[:, c0:c0 + NSC], S[:, ri, :]), [S], [hS])
                        mm(pYsm[:, 0:NSC], wcz[ri][:, ql * 128:(ql + 1) * 128], hS[:, c0:c0 + NSC], (ql == 0 and ri == 0), (ql == 3 and ri == 1),
                           r=[wcz[ri], hS], w=[pYsm])
                        dv(lambda e: e.tensor_copy(H0t[:, ri, ql, :], S[:, ri, :].rearrange("p (s t) -> p s t", t=TS)[:, :, TS - 1]), [S], [H0t])
            if b.has_s:
                for ri, dst in enumerate((hre_s, him_s)):
                    p = psget(0, 4)
                    for ql in range(4):
                        transpose(p[0:16, ql * 128:(ql + 1) * 128], H0t[:, ri, ql, :], ident[:], r=[H0t, ident], w=[p])
                    dv(lambda e: e.tensor_copy(H0in[0:16, ri, :], p[0:16, 0:512]), [p], [H0in])
                    fw.dma("sp", dst[l][:, 8 * ft:8 * ft + 8, :].rearrange("s g p -> s (g p)"), H0in[0:16, ri, :], owner=H0in, r=[H0in])
            dcol = vecT[l][:, V_SD + ft:V_SD + ft + 1]
            glist = [(gi, c0, n, pYs[gi]) for gi, (c0, n) in enumerate(pgroups)] + ([(2, npr, NSC, pYsm)] if b.has_s else [])
            for (gi, c0, n, py) in glist:
                t = tmp(); t2 = tmp()
                dv(lambda e: e.scalar_tensor_tensor(t[:, 0:n], uT[:, c0:c0 + n], dcol, py[:, 0:n], ALU.mult, ALU.add), [uT, vecT[l], py], [t])
                dv(lambda e: e.tensor_tensor(t2[:, 0:n], t[:, 0:n], t[:, 0:n], ALU.mult), [t], [t2])
                dv(lambda e: e.tensor_scalar(t2[:, 0:n], t2[:, 0:n], 0.044715, 1.0, ALU.mult, ALU.add), [t2], [t2])
                dv(lambda e: e.tensor_tensor(t2[:, 0:n], t2[:, 0:n], t[:, 0:n], ALU.mult), [t2, t], [t2])
                fw.op("act", lambda e: e.activation(out=t2[:, 0:n], in_=t2[:, 0:n], func=AF.Sigmoid, scale=2.0 * math.sqrt(2.0 / math.pi)), r=[t2], w=[t2])
                dv(lambda e: e.tensor_tensor(zT[ft][:, c0:c0 + n], t2[:, 0:n], t[:, 0:n], ALU.mult), [t2, t], [zT[ft]])
        if b.i == 2:
            for ri, dst in enumerate((hre_p, him_p)):
                p = psget(0, 4)
                transpose(p[0:32, 0:128], hfin[:, ri, :], ident[:], r=[hfin, ident], w=[p])
                dv(lambda e: e.tensor_copy(H0in[0:32, ri, 0:128], p[0:32, 0:128]), [p], [H0in])
                fw.dma("sp", dst[l].rearrange("(q g2) p -> q (g2 p)", g2=2), H0in[0:32, ri, 0:128], owner=H0in, r=[H0in])
        for m in range(KT):
            cg = ws.add(lambda t_, m=m: [(vg(t_)[:, :, 0:128], w_glu[l][:, m * 128:(m + 1) * 128].rearrange("(k p) n -> p k n", p=128)),
                                         (vg(t_)[:, :, 128:256], w_glu[l][:, D + m * 128:D + (m + 1) * 128].rearrange("(k p) n -> p k n", p=128)),
                                         (vg(t_)[:, :, 256:384], w_in[l][:, OFF_GB + m * 128:OFF_GB + (m + 1) * 128].rearrange("(k p) n -> p k n", p=128))])
            wtt = ws.get(cg, ahead=0); wt = vg(wtt)
            for (c0, n) in b.groups:
                pa = psget(0, 4); pb = psget(0, 4); pc = psget(4, 8)
                for k in range(KT):
                    mm(pa[:, 0:n], wt[:, k, 0:128], zT[k][:, c0:c0 + n], k == 0, k == KT - 1, r=[wtt, zT[k]], w=[pa])
                for k in range(KT):
                    mm(pb[:, 0:n], wt[:, k, 128:256], zT[k][:, c0:c0 + n], k == 0, k == KT - 1, r=[wtt, zT[k]], w=[pb])
                for k in range(KT):
                    mm(pc[:, 0:n], wt[:, k, 256:384], hT[k][:, c0:c0 + n], k == 0, k == KT - 1, r=[wtt, hT[k]], w=[pc])
                t = tmp(); t2 = tmp()
                fw.op("act", lambda e: e.activation(out=t[:, 0:n], in_=pb[:, 0:n], func=AF.Sigmoid), r=[pb], w=[t])
                fw.op("act", lambda e: e.activation(out=t2[:, 0:n], in_=pc[:, 0:n], func=AF.Sigmoid), r=[pc], w=[t2])
                dv(lambda e: e.tensor_tensor(t[:, 0:n], t[:, 0:n], pa[:, 0:n], ALU.mult), [t, pa], [t])
                dv(lambda e: e.tensor_tensor(t[:, 0:n], t[:, 0:n], t2[:, 0:n], ALU.mult), [t, t2], [t])
                dv(lambda e: e.tensor_tensor(mixed[m][:, c0:c0 + n], mixed[m][:, c0:c0 + n], t[:, 0:n], ALU.add), [mixed[m], t], [mixed[m]])

    def final_out(b):
        rms_stats(b, [(xT[k], 128, xT[k]) for k in range(KT)], KT, D)
        yT = fw_cqf
        ntt = (b.ncols + 127) // 128
        for tt in range(ntt):
            t0 = tt * 128
            nt = min(128, b.ncols - t0)
            pA, pB = PS[4 + (tt % 2) * 2], PS[5 + (tt % 2) * 2]
            for k in range(KT):
                t = tmp()
                fw.op("dve", lambda e: e.scalar_tensor_tensor(t[:, 0:nt], xT[k][:, t0:t0 + nt], vecT[0][:, V_FN + k:V_FN + k + 1],
                                                             rstd[:, t0:t0 + nt], ALU.mult, ALU.mult), r=[xT[k], vecT[0], rstd], w=[t])
                pp = pA if k < 4 else pB
                transpose(pp[0:nt, (k % 4) * 128:(k % 4 + 1) * 128], t[:, 0:nt], ident[:], r=[t, ident], w=[pp])
            st = stage[tt % 2]
            fw.op("act", lambda e: e.activation(out=st[0:nt, 0:512], in_=pA[0:nt, :], func=AF.Copy), r=[pA], w=[st])
            fw.op("dve", lambda e: e.tensor_copy(st[0:nt, 512:1024], pB[0:nt, :]), r=[pB], w=[st])
            if t0 < b.npr:
                fw.dma("sp", y_p[b.p0 + t0:b.p0 + t0 + nt, :], st[0:nt, :], owner=st, r=[st])
            else:
                fw.dma("sp", y_s[0:nt, :], st[0:nt, :], owner=st, r=[st])

    if cfg.get('sa_setup', True):
        setup_idx()
    fw.dma("sp", s5k[:], s5c_d[:, 768:870], owner=s5k, w=[s5k])
    fw.op("dve", lambda e: e.memset(onesf[:], 1.0), w=[onesf])
    fw.op("dve", lambda e: e.memset(npi[:], 0.5 * math.pi), w=[npi])
    for l in L_RANGE:
        s5_params(l)
    for l in L_RANGE:
        if cfg.get("vecs", True):
            load_vecs(l)
        if cfg.get("mod", True):
            compute_mod(l)
    for bi in BLOCKS:
        b = mkblk(bi)
        if cfg.get("loadx", True):
            load_x(b)
        for l in L_RANGE:
            if cfg.get("ffn1", True):
                ffn(b, l, 0, ffn1_up, ffn1_down, 0)
            if cfg.get("mixer", True):
                mixer(b, l)
            if cfg.get("ffn2", True):
                ffn(b, l, 2, ffn2_up, ffn2_down, 2)
        if cfg.get("final", True):
            final_out(b)
    fw.finish()
    es.close()
    print("instructions:", fw.nins, "dma sems:", len(fw.dsems))
    return nc


def host_consts():
    ident = np.eye(128, dtype=np.float32)
    tri = np.triu(np.ones((128, 128), dtype=np.float32))
    half = ROPE // 2
    inv = (np.float32(10000.0) ** (-np.arange(half, dtype=np.float32) / np.float32(half))).astype(np.float32)
    pos = np.concatenate([np.arange(TP), np.tile(PAST + np.arange(TS), NSQ)]).astype(np.float32)
    ang = (pos[None, :] * inv[:, None]).astype(np.float32)
    cos = np.cos(ang).astype(np.float32); sin = np.sin(ang).astype(np.float32)
    rope = np.stack([np.concatenate([cos, cos], 0), np.concatenate([sin, sin], 0)]).astype(np.float32)
    mnew = np.zeros((64, NSQ, 32), np.float32)
    for sq in range(NSQ):
        for t in range(TS):
            for t2 in range(t + 1):
                mnew[sq * TS + t2, sq, t::TS] = 1.0
    ecol = (np.arange(128) // 8).astype(np.float32).reshape(128, 1)
    s5c = np.zeros((128, 870), np.float32)
    s5c[:, 0:768] = np.arange(768, dtype=np.float32)[None, :]
    pidx = np.arange(128)
    for g2 in range(2):
        s5c[:, 768 + g2] = (pidx // 64 == g2)
    for ql in range(4):
        s5c[:, 770 + ql] = (pidx // 32 == ql)
    for ql in range(4):
        for a in range(4):
            for g2 in range(2):
                s5c[:, 774 + ql * 8 + a * 2 + g2] = (a == ql) * (pidx // 64 == g2)
    s5c[:, 806:870] = (np.arange(64) % 4 != 0).astype(np.float32)[None, :]
    return ident, tri, rope, mnew, ecol, s5c


_NAMES = ["ada_w", "ada_b", "norm_ffn1", "ffn1_up", "ffn1_down", "norm_mix", "w_in", "q_norm", "w_uq", "kv_norm", "w_uk", "w_uv",
          "ssm_a_re", "ssm_a_im", "ssm_log_dt", "ssm_b_re", "ssm_b_im", "ssm_c_re", "ssm_c_im", "ssm_d", "w_glu", "w_out",
          "norm_ffn2", "ffn2_up", "ffn2_down", "final_norm"]


def make_in_maps(inputs, cores):
    ident, tri, rope, mnew, ecol, s5c = host_consts()
    maps = []
    shared = {k: np.ascontiguousarray(inputs[k]) for k in _NAMES}
    for i in range(2):
        shared["cache_ckv%d" % i] = np.ascontiguousarray(inputs["cache_ckv"][i])
        shared["cache_kr%d" % i] = np.ascontiguousarray(inputs["cache_kr"][i])
    for c in cores:
        m = dict(shared)
        m["xp"] = np.ascontiguousarray(inputs["x_prompt"][c])
        m["xs"] = np.ascontiguousarray(inputs["x_sample"][c * NSQ:(c + 1) * NSQ].reshape(NSC, D))
        m["cvec"] = np.ascontiguousarray(np.concatenate([inputs["c_prompt"][c:c + 1], inputs["c_sample"][c * NSQ:(c + 1) * NSQ]], 0))
        m["st_re"] = np.ascontiguousarray(inputs["state_ssm_re"][:, c * NSQ:(c + 1) * NSQ])
        m["st_im"] = np.ascontiguousarray(inputs["state_ssm_im"][:, c * NSQ:(c + 1) * NSQ])
        m["ptab"] = np.ascontiguousarray(inputs["page_table"][c * NSQ:(c + 1) * NSQ]).astype(np.int32)
        m["ident"] = ident; m["tri"] = tri; m["rope"] = rope; m["mnew"] = mnew; m["ecol"] = ecol; m["s5c"] = s5c
        maps.append(m)
    return maps


def assemble(results, ncores=8):
    def cat(name, axis):
        return np.concatenate([np.asarray(r[name]) for r in results], axis=axis)
    y_p = np.stack([r["y_p"] for r in results], 0)
    y_s = np.concatenate([r["y_s"].reshape(NSQ, TS, D) for r in results], 0)
    ckv_p = np.stack([r["ckv_p"] for r in results], 1)
    kr_p = np.stack([r["kr_p"] for r in results], 1)
    hre_p = np.stack([r["hre_p"] for r in results], 1)
    him_p = np.stack([r["him_p"] for r in results], 1)
    ckv_s = np.concatenate([r["ckv_s"].reshape(2, NSQ, TS, LAT) for r in results], 1)
    kr_s = np.concatenate([r["kr_s"].reshape(2, NSQ, TS, ROPE) for r in results], 1)
    hre_s = np.concatenate([r["hre_s"] for r in results], 1)
    him_s = np.concatenate([r["him_s"] for r in results], 1)
    return tuple(np.ascontiguousarray(a.astype(np.float32)) for a in
                 (y_p, y_s, ckv_p, kr_p, hre_p, him_p, ckv_s, kr_s, hre_s, him_s))


def kernel(**inputs):
    inputs = {k: np.asarray(v) for k, v in inputs.items()}
    nc = build_program({})
    maps = make_in_maps(inputs, list(range(8)))
    res = run_bass_kernel_spmd(nc, maps, core_ids=list(range(8)))
    return assemble(res.results)
```
